# Optimizing a Trainium2 kernel written in Bass

```python
import math
import jax
import jax.numpy as jnp
from jax import lax
import numpy as np

D_MODEL = 2048
BATCH = 4
SEQ = 4096
DEPTH = 2

GRID_W = 64
CTX_LEN = 256
EPS = 1e-6
N_MOD = 6
D_FF = -(-8 * D_MODEL // (3 * 256)) * 256
N_EVEN = (DEPTH + 1) // 2
N_ODD = DEPTH // 2
A_HEADS = D_MODEL // 256
A_QK_DIM = 64
A_V_DIM = 2 * A_QK_DIM
ROPE_HALF = A_QK_DIM // 2
ROPE_FREQS = ROPE_HALF // 2
ROPE_BASE = 10000.0
Q_BLOCK = 128
B_GROUPS = D_MODEL // 256
B_CH = 128
CHUNK = 128
C_CH = D_MODEL // 2
CONV_W = 31
D_GROUPS = D_MODEL // 256
D_CH = 128
A_Q_COLS = A_HEADS * 2 * A_QK_DIM
A_V_COLS = A_HEADS * A_V_DIM
B_COLS = B_GROUPS * B_CH
AB_IN = 2 * A_Q_COLS + A_V_COLS + 2 * B_COLS
AB_MIX = A_V_COLS + B_COLS
D_COLS = D_GROUPS * D_CH
CD_IN = 2 * C_CH + D_COLS
CD_MIX = C_CH + D_COLS

kernel_name = 'hybrid_diffattn_gmlp_conformer_fnet_dit'


def rms_norm(x, g=None):
    xf = x.astype(jnp.float32)
    y = xf * lax.rsqrt(jnp.mean(xf * xf, axis=-1, keepdims=True) + EPS)
    if g is not None:
        y = y * g.astype(jnp.float32)
    return y.astype(x.dtype)


def layer_norm(x, g, b):
    xf = x.astype(jnp.float32)
    mu = jnp.mean(xf, axis=-1, keepdims=True)
    var = jnp.mean(jnp.square(xf - mu), axis=-1, keepdims=True)
    y = (xf - mu) * lax.rsqrt(var + EPS) * g.astype(jnp.float32) + b.astype(jnp.float32)
    return y.astype(x.dtype)


def modulate(h, shift, scale):
    return h * (1 + scale) + shift


def grid_positions(n):
    n_rows = n // GRID_W
    rows = jnp.repeat(jnp.arange(n_rows, dtype=jnp.float32), GRID_W)
    cols = jnp.tile(jnp.arange(GRID_W, dtype=jnp.float32), n_rows)
    return rows, cols


def _rotate(t, ang):
    cos = jnp.cos(ang).astype(t.dtype)[:, None, None, :]
    sin = jnp.sin(ang).astype(t.dtype)[:, None, None, :]
    t1, t2 = t[..., :ROPE_FREQS], t[..., ROPE_FREQS:]
    return jnp.concatenate([t1 * cos - t2 * sin, t1 * sin + t2 * cos], axis=-1)


def axial_rope(t, rows, cols):
    inv = ROPE_BASE ** (-jnp.arange(ROPE_FREQS, dtype=jnp.float32) / ROPE_FREQS)
    return jnp.concatenate([_rotate(t[..., :ROPE_HALF], rows[:, None] * inv),
                            _rotate(t[..., ROPE_HALF:], cols[:, None] * inv)], axis=-1)


def diff_softmax_attend(q, k, v, lam):
    s = jnp.einsum('bqhtd,bkhtd->bhtqk', q, k).astype(jnp.float32) * (A_QK_DIM ** -0.5)
    p = jax.nn.softmax(s, axis=-1)
    a = p[:, :, 0] - lam * p[:, :, 1]
    return jnp.einsum('bhqk,bkhd->bqhd', a.astype(v.dtype), v)


def latent_diff_attention(q, k_all, v_all, lam):
    b, n = q.shape[0], q.shape[1]
    nb = n // Q_BLOCK
    qb = jnp.moveaxis(q.reshape(b, nb, Q_BLOCK, A_HEADS, 2, A_QK_DIM), 1, 0)
    o = lax.map(lambda qq: diff_softmax_attend(qq, k_all, v_all, lam), qb)
    return jnp.moveaxis(o, 0, 1).reshape(b, n, A_HEADS, A_V_DIM)


def spatial_gate(z, vnorm_g, vnorm_b, w_spatial, b_spatial):
    b, n = z.shape[0], z.shape[1]
    u, v = z[..., :B_COLS], z[..., B_COLS:]
    v = layer_norm(v.reshape(b, n, B_GROUPS, B_CH),
                   vnorm_g.reshape(B_GROUPS, B_CH), vnorm_b.reshape(B_GROUPS, B_CH))
    v = v.reshape(b, n // CHUNK, CHUNK, B_GROUPS, B_CH)
    sv = jnp.einsum('gpq,bnqgc->bnpgc', w_spatial, v) + b_spatial.T[:, :, None]
    return u * sv.reshape(b, n, B_COLS)


def mixer_ab(h_lat, h_ctx, ctx_out, layer_idx, w_in, w_out, lam_q1, lam_k1, lam_q2, lam_k2,
             subln_g, vnorm_g, vnorm_b, w_spatial, b_spatial):
    b, n, _ = h_lat.shape
    m = h_ctx.shape[1]
    lam_init = 0.8 - 0.6 * math.exp(-0.3 * layer_idx)
    f32 = jnp.float32
    lam = (jnp.exp(jnp.sum(lam_q1.astype(f32) * lam_k1.astype(f32)))
           - jnp.exp(jnp.sum(lam_q2.astype(f32) * lam_k2.astype(f32))) + lam_init)
    i_k, i_v, i_b = A_Q_COLS, 2 * A_Q_COLS, 2 * A_Q_COLS + A_V_COLS
    z = h_lat @ w_in
    rows, cols = grid_positions(n)
    q = axial_rope(z[..., :i_k].reshape(b, n, A_HEADS, 2, A_QK_DIM), rows, cols)
    k = axial_rope(z[..., i_k:i_v].reshape(b, n, A_HEADS, 2, A_QK_DIM), rows, cols)
    v = z[..., i_v:i_b].reshape(b, n, A_HEADS, A_V_DIM)
    if ctx_out:
        zc = h_ctx @ w_in
        zc_kv = zc[..., i_k:i_b]
    else:
        zc_kv = h_ctx @ w_in[:, i_k:i_b]
    kc = zc_kv[..., :A_Q_COLS].reshape(b, m, A_HEADS, 2, A_QK_DIM)
    vc = zc_kv[..., A_Q_COLS:].reshape(b, m, A_HEADS, A_V_DIM)
    k_all = jnp.concatenate([k, kc], axis=1)
    v_all = jnp.concatenate([v, vc], axis=1)
    o = latent_diff_attention(q, k_all, v_all, lam)
    a_lat = (rms_norm(o, subln_g) * (1 - lam_init)).reshape(b, n, A_V_COLS)
    s_lat = spatial_gate(jax.nn.gelu(z[..., i_b:]), vnorm_g, vnorm_b, w_spatial, b_spatial)
    y_lat = jnp.concatenate([a_lat, s_lat], axis=-1) @ w_out
    y_ctx = None
    if ctx_out:
        qc = zc[..., :i_k].reshape(b, m, A_HEADS, 2, A_QK_DIM)
        oc = diff_softmax_attend(qc, kc, vc, lam)
        a_ctx = (rms_norm(oc, subln_g) * (1 - lam_init)).reshape(b, m, A_V_COLS)
        s_ctx = spatial_gate(jax.nn.gelu(zc[..., i_b:]), vnorm_g, vnorm_b, w_spatial, b_spatial)
        y_ctx = jnp.concatenate([a_ctx, s_ctx], axis=-1) @ w_out
    return y_lat, y_ctx


def conformer_conv(z, dw_w, dw_b, norm_g, norm_b):
    y = z[..., :C_CH] * jax.nn.sigmoid(z[..., C_CH:])
    pad = (CONV_W - 1) // 2
    y = lax.conv_general_dilated(y, dw_w[:, None, :], window_strides=(1,), padding=[(pad, pad)],
                                 dimension_numbers=('NWC', 'WIO', 'NWC'),
                                 feature_group_count=C_CH) + dw_b
    return jax.nn.silu(layer_norm(y, norm_g, norm_b))


def fourier_mix(f):
    b, n = f.shape[0], f.shape[1]
    fg = f.reshape(b, n, D_GROUPS, D_CH).astype(jnp.float32)
    out = jnp.real(jnp.fft.fft2(fg, axes=(1, 3), norm='ortho'))
    return out.astype(f.dtype).reshape(b, n, D_COLS)


def mixer_cd(h_lat, h_ctx, w_in, w_out, dw_w, dw_b, norm_g, norm_b):
    def one_sequence(h):
        z = h @ w_in
        yc = conformer_conv(z[..., :2 * C_CH], dw_w, dw_b, norm_g, norm_b)
        yd = fourier_mix(z[..., 2 * C_CH:])
        return jnp.concatenate([yc, yd], axis=-1) @ w_out
    y_lat = one_sequence(h_lat)
    y_ctx = one_sequence(h_ctx) if h_ctx is not None else None
    return y_lat, y_ctx


def swiglu(h, w_gate, w_up, w_down):
    return (jax.nn.silu(h @ w_gate) * (h @ w_up)) @ w_down


def setup_inputs(seed: int = 0) -> dict:
    key = jax.random.key(seed)
    ks = iter(jax.random.split(key, 40))
    nrm = lambda shape, s: jax.random.normal(next(ks), shape, jnp.float32) * s
    D = D_MODEL
    return {
        'x': nrm((BATCH, SEQ, D), 1.0),
        'c': nrm((BATCH, D), 1.0),
        'ctx': nrm((BATCH, CTX_LEN, D), 1.0),
        'c_ctx': nrm((D,), 1.0),
        'mod_w': nrm((DEPTH, D, N_MOD * D), D ** -0.5),
        'mod_b': nrm((DEPTH, N_MOD * D), 0.02),
        'post_mix_g': 1.0 + nrm((DEPTH, D), 0.02),
        'post_ffn_g': 1.0 + nrm((DEPTH, D), 0.02),
        'ffn_w_gate': nrm((DEPTH, D, D_FF), D ** -0.5),
        'ffn_w_up': nrm((DEPTH, D, D_FF), D ** -0.5),
        'ffn_w_down': nrm((DEPTH, D_FF, D), D_FF ** -0.5),
        'ab_w_in': nrm((N_EVEN, D, AB_IN), D ** -0.5),
        'ab_w_out': nrm((N_EVEN, AB_MIX, D), AB_MIX ** -0.5),
        'ab_lam_q1': nrm((N_EVEN, A_QK_DIM), 0.1),
        'ab_lam_k1': nrm((N_EVEN, A_QK_DIM), 0.1),
        'ab_lam_q2': nrm((N_EVEN, A_QK_DIM), 0.1),
        'ab_lam_k2': nrm((N_EVEN, A_QK_DIM), 0.1),
        'ab_subln_g': 1.0 + nrm((N_EVEN, A_V_DIM), 0.02),
        'ab_vnorm_g': 1.0 + nrm((N_EVEN, B_COLS), 0.02),
        'ab_vnorm_b': nrm((N_EVEN, B_COLS), 0.02),
        'ab_w_spatial': nrm((N_EVEN, B_GROUPS, CHUNK, CHUNK), CHUNK ** -0.5),
        'ab_b_spatial': 1.0 + nrm((N_EVEN, B_GROUPS, CHUNK), 0.1),
        'cd_w_in': nrm((N_ODD, D, CD_IN), D ** -0.5),
        'cd_w_out': nrm((N_ODD, CD_MIX, D), CD_MIX ** -0.5),
        'cd_dw_w': nrm((N_ODD, CONV_W, C_CH), CONV_W ** -0.5),
        'cd_dw_b': nrm((N_ODD, C_CH), 0.02),
        'cd_norm_g': 1.0 + nrm((N_ODD, C_CH), 0.02),
        'cd_norm_b': nrm((N_ODD, C_CH), 0.02),
    }


def reference(x, c, ctx, c_ctx, mod_w, mod_b, post_mix_g, post_ffn_g, ffn_w_gate, ffn_w_up,
              ffn_w_down, ab_w_in, ab_w_out, ab_lam_q1, ab_lam_k1, ab_lam_q2, ab_lam_k2,
              ab_subln_g, ab_vnorm_g, ab_vnorm_b, ab_w_spatial, ab_b_spatial, cd_w_in, cd_w_out,
              cd_dw_w, cd_dw_b, cd_norm_g, cd_norm_b):
    x_lat, x_ctx = x, ctx
    silu_c = jax.nn.silu(c)
    silu_cc = jax.nn.silu(c_ctx)
    for l in range(DEPTH):
        last = l == DEPTH - 1
        even = l % 2 == 0
        i = l // 2
        mod = (silu_c @ mod_w[l] + mod_b[l])[:, None, :]
        sh_m, sc_m, g_m, sh_f, sc_f, g_f = jnp.split(mod, N_MOD, axis=-1)
        h_lat = modulate(rms_norm(x_lat), sh_m, sc_m)
        h_ctx = None
        if (not last) or even:
            mod_c = silu_cc @ mod_w[l] + mod_b[l]
            csh_m, csc_m, cg_m, csh_f, csc_f, cg_f = jnp.split(mod_c, N_MOD, axis=-1)
            h_ctx = modulate(rms_norm(x_ctx), csh_m, csc_m)
        if even:
            y_lat, y_ctx = mixer_ab(h_lat, h_ctx, not last, l, ab_w_in[i], ab_w_out[i],
                                    ab_lam_q1[i], ab_lam_k1[i], ab_lam_q2[i], ab_lam_k2[i],
                                    ab_subln_g[i], ab_vnorm_g[i], ab_vnorm_b[i],
                                    ab_w_spatial[i], ab_b_spatial[i])
        else:
            y_lat, y_ctx = mixer_cd(h_lat, h_ctx, cd_w_in[i], cd_w_out[i], cd_dw_w[i],
                                    cd_dw_b[i], cd_norm_g[i], cd_norm_b[i])
        x_lat = x_lat + g_m * rms_norm(y_lat, post_mix_g[l])
        f_lat = swiglu(modulate(rms_norm(x_lat), sh_f, sc_f), ffn_w_gate[l], ffn_w_up[l], ffn_w_down[l])
        x_lat = x_lat + g_f * rms_norm(f_lat, post_ffn_g[l])
        if not last:
            x_ctx = x_ctx + cg_m * rms_norm(y_ctx, post_mix_g[l])
            f_ctx = swiglu(modulate(rms_norm(x_ctx), csh_f, csc_f), ffn_w_gate[l], ffn_w_up[l], ffn_w_down[l])
            x_ctx = x_ctx + cg_f * rms_norm(f_ctx, post_ffn_g[l])
    return x_lat
```

```python
import numpy as np
from contextlib import ExitStack
import ml_dtypes
import concourse.bass as bass
import concourse.mybir as mybir
from concourse.bass_utils import run_bass_kernel_spmd

F32, BF16 = mybir.dt.float32, mybir.dt.bfloat16
AF = mybir.ActivationFunctionType
ALU = mybir.AluOpType
AX = mybir.AxisListType

D = 2048
DC = 16
DFF = 5632
FC = 44
CTX = 256
EPS = 1e-6
GRID_W = 64
EPOCH = 30000
TB = 512
import os
CUT = int(os.environ.get('KCUT', '99'))
KV = os.environ.get('KV', '')


class Res:
    __slots__ = ("name", "w", "r", "dsem", "dcnt")

    def __init__(self, name):
        self.name = name
        self.w = None
        self.r = {}
        self.dsem = None
        self.dcnt = 0


class Queue:
    def __init__(self, name, h):
        self.name = name
        self.h = h
        self.n = 0
        self.sems = []
        self.known = {}


class Sched:
    def __init__(self, nc):
        self.nc = nc
        self.q = {
            "pe": Queue("pe", nc.tensor),
            "act": Queue("act", nc.scalar),
            "dve": Queue("dve", nc.vector),
            "pool": Queue("pool", nc.gpsimd),
            "sp": Queue("sp", nc.sync),
        }
        self.dres = []
        self.nsem = 0

    def _sem(self, name):
        self.nsem += 1
        return self.nc.alloc_semaphore(name)

    def _qsem(self, q, idx):
        e = idx // EPOCH
        while len(q.sems) <= e:
            q.sems.append(self._sem(f"q{q.name}{len(q.sems)}"))
        return q.sems[e], (idx % EPOCH) + 1, e

    def _wait(self, q, ev):
        if ev[0] == "q":
            _, qn, idx = ev
            if qn == q.name and qn in ("pe", "sp"):
                return
            src = self.q[qn]
            sem, val, e = self._qsem(src, idx)
            for (kq, ke), kv in q.known.items():
                if kq == qn and (ke > e or (ke == e and kv >= val)):
                    return
            q.h.wait_ge(sem, val)
            q.known[(qn, e)] = val
        else:
            res = ev[1]
            key = ("d", id(res))
            if q.known.get((key, 0), 0) >= res.dcnt:
                return
            q.h.wait_ge(res.dsem, res.dcnt)
            q.known[(key, 0)] = res.dcnt

    def _deps(self, q, reads, writes):
        for r in reads:
            if r.w is not None:
                self._wait(q, r.w)
        for w in writes:
            if w.w is not None:
                self._wait(q, w.w)
            for ev in list(w.r.values()):
                self._wait(q, ev)

    def op(self, qn, fn, reads=(), writes=()):
        q = self.q[qn]
        self._deps(q, reads, writes)
        sem, val, _ = self._qsem(q, q.n)
        ins = fn(q.h)
        ins.then_inc(sem, 1)
        me = ("q", qn, q.n)
        q.n += 1
        for r in reads:
            r.r[qn] = me
        for w in writes:
            w.w = me
            w.r = {}

    def dma(self, qn, out, in_, reads, writes, tag):
        q = self.q[qn]
        self._deps(q, reads, writes)
        if tag.dsem is None:
            tag.dsem = self._sem("d" + tag.name)
            self.dres.append(tag)
        ins = q.h.dma_start(out=out, in_=in_)
        ins.then_inc(tag.dsem, 16)
        tag.dcnt += 16
        me = ("d", tag)
        for r in reads:
            r.r[("d", id(tag))] = me
        for w in writes:
            w.w = me
            w.r = {}

    def barrier(self):
        for q in self.q.values():
            for o in self.q.values():
                if o is not q and o.n > 0:
                    self._wait(q, ("q", o.name, o.n - 1))
            for r in self.dres:
                self._wait(q, ("d", r))


class T:
    def __init__(self, h, name):
        self.t = h
        self.res = Res(name)
        self.name = name

    def __getitem__(self, k):
        return self.t[k]


def build(Tn, dbg=False, only=None):
    NB = Tn // TB
    NT = Tn // 128
    NK = Tn + CTX
    NKT = NK // 128
    nc = bass.Bass("TRN2", target_bir_lowering=False)
    S = Sched(nc)

    def din(name, shape, dt=F32):
        return nc.dram_tensor(name, list(shape), dt, kind="ExternalInput").ap()

    def dscr(name, shape, dt):
        kind = "ExternalOutput" if dbg else "Internal"
        return nc.dram_tensor(name, list(shape), dt, kind=kind).ap()

    x_d = din("x", [Tn, D])
    ctx_d = din("ctx", [CTX, D])
    cvec_d = din("cvec", [128, DC, 2])
    modw_d = din("mod_w", [2, D, 6 * D])
    modb_d = din("modb", [128, 2, 96])
    pmg_d = din("pmg", [128, 2, DC])
    pfg_d = din("pfg", [128, 2, DC])
    wg_d = din("ffn_w_gate", [2, D, DFF])
    wu_d = din("ffn_w_up", [2, D, DFF])
    wd_d = din("ffn_w_down", [2, DFF, D])
    abin_d = din("ab_w_in", [1, D, 5120])
    about_d = din("ab_w_out", [1, D, D])
    cdin_d = din("cd_w_in", [1, D, 3072])
    cdout_d = din("cd_w_out", [1, D, D])
    lamv_d = din("lamv", [128, 4, 64])
    subg_d = din("subg", [128, 1])
    vng_d = din("vng", [128, 1024])
    vnb_d = din("vnb", [128, 1024])
    bsp_d = din("bsp", [128, 8, 128])
    wsT_d = din("wsT", [128, 8, 128])
    dww_d = din("dww", [128, 8, 31])
    dwb_d = din("dwb", [128, 8])
    cng_d = din("cng", [128, 8])
    cnb_d = din("cnb", [128, 8])
    ident_d = din("ident", [128, 128])
    permT_d = din("permT", [128, 128])
    ropeC_d = din("ropeC", [128, Tn])
    ropeS_d = din("ropeS", [128, Tn])
    ccs_d = din("ccs", [128, 256])
    dft_d = din("dft", [NB, 128, NT, 2, TB], BF16)
    out_d = nc.dram_tensor("out", [Tn, D], F32, kind="ExternalOutput").ap()

    xT_d = dscr("xT_s", [NB, 128, DC, TB], F32)
    qT_d = dscr("qT_s", [8, 2, 64, Tn], BF16)
    kT_d = dscr("kT_s", [8, 2, 64, NK], BF16)
    vS_d = dscr("vS_s", [NK, 1024], BF16)
    mixT_d = dscr("mixT_s", [NB, 128, DC, TB], BF16)
    G_d = dscr("G_s", [8, 128, Tn + 32], F32)
    GG_d = dscr("GG_s", [8, 128, NT, 256], BF16)
    wA_b = nc.dram_tensor("wA_b", [10, 128, DC, 512], BF16, kind="Internal").ap()
    wC_b = nc.dram_tensor("wC_b", [24, 128, DC, 128], BF16, kind="Internal").ap()
    wo_b = [nc.dram_tensor(f"wo_b{l}", [DC, 128, DC, 128], BF16, kind="Internal").ap() for l in range(2)]
    wg_b = [nc.dram_tensor(f"wg_b{l}", [FC, 128, DC, 128], BF16, kind="Internal").ap() for l in range(2)]
    wu_b = [nc.dram_tensor(f"wu_b{l}", [FC, 128, DC, 128], BF16, kind="Internal").ap() for l in range(2)]
    wd_b = [nc.dram_tensor(f"wd_b{l}", [DC, 128, FC, 128], BF16, kind="Internal").ap() for l in range(2)]
    r_wA, r_wC = Res("wA"), Res("wC")
    r_wB = [Res("wB0"), Res("wB1")]
    r_xT = [Res(f"xT{b}") for b in range(NB)]
    r_mixT = [Res(f"mixT{b}") for b in range(NB)]
    r_qT, r_kT, r_vS, r_G, r_GG = Res("qT"), Res("kT"), Res("vS"), Res("G"), Res("GG")
    r_out = Res("out")

    stacks = [ExitStack()]

    def tile(name, shape, dt, psum=False):
        if psum:
            return T(nc.alloc_psum_tensor("ps_" + name, list(shape), dt), name)
        return T(stacks[-1].enter_context(nc.sbuf_tensor("sb_" + name, list(shape), dt)), name)

    class BankView:
        def __init__(self, base, off, name):
            self.base, self.off, self.name = base, off, name
            self.res = Res(name)

        def __getitem__(self, k):
            p, c = k
            a = 0 if c.start is None else c.start
            b = 512 if c.stop is None else c.stop
            return self.base[p, self.off + a:self.off + b]

    dbl = [nc.alloc_psum_tensor(f"ps_dbl{i}", [128, 1024], F32) for i in range(4)]
    banks = [BankView(dbl[i // 2], (i % 2) * 512, f"bank{i}") for i in range(8)]

    def mm(bank, out, lhsT, rhs, start, stop, reads):
        S.op("pe", lambda e: e.matmul(out, lhsT=lhsT, rhs=rhs, start=start, stop=stop), [x.res for x in reads], [bank.res])

    def act(out, in_, func, reads, writes, scale=1.0, bias=0.0):
        S.op("act", lambda e: e.activation(out=out, in_=in_, func=func, bias=bias, scale=scale),
             [x.res for x in reads], [x.res for x in writes])

    def tt(eng, out, in0, in1, op, reads, writes):
        S.op(eng, lambda e: e.tensor_tensor(out=out, in0=in0, in1=in1, op=op),
             [x.res for x in reads], [x.res for x in writes])

    def ts(eng, out, in0, s1, s2, op0, op1, reads, writes):
        if op1 is None:
            S.op(eng, lambda e: e.tensor_scalar(out=out, in0=in0, scalar1=s1, scalar2=None, op0=op0),
                 [x.res for x in reads], [x.res for x in writes])
        else:
            S.op(eng, lambda e: e.tensor_scalar(out=out, in0=in0, scalar1=s1, scalar2=s2, op0=op0, op1=op1),
                 [x.res for x in reads], [x.res for x in writes])

    def stt(out, in0, scalar, in1, op0, op1, reads, writes):
        S.op("dve", lambda e: e.scalar_tensor_tensor(out=out, in0=in0, scalar=scalar, in1=in1, op0=op0, op1=op1),
             [x.res for x in reads], [x.res for x in writes])

    def recip(out, in_, reads, writes):
        S.op("dve", lambda e: e.reciprocal(out=out, in_=in_), [x.res for x in reads], [x.res for x in writes])

    def load(qn, dst, out_ap, in_ap, rres=()):
        S.dma(qn, out_ap, in_ap, list(rres), [dst.res], dst.res)

    def store(src, out_ap, in_ap, wres):
        S.dma("sp", out_ap, in_ap, [src.res], list(wres), src.res)

    ones_bf = tile("ones_bf", [128, 128], BF16)
    ident = tile("ident", [128, 128], F32)
    permT = tile("permT", [128, 128], BF16)
    S.op("dve", lambda e: e.memset(ones_bf[:, :], 1.0), [], [ones_bf.res])
    load("sp", ident, ident[:, :], ident_d[:, :])
    load("pool", permT, permT[:, :], permT_d[:, :])
    cvec = tile("cvec", [128, DC, 2], F32)
    scv = tile("scv", [128, DC, 2], F32)
    load("sp", cvec, cvec[:, :, :], cvec_d[:, :, :])
    act(scv[:, :, :], cvec[:, :, :], AF.Silu, [cvec], [scv])
    modb = tile("modb", [128, 2, 96], F32)
    pmg = tile("pmg", [128, 2, DC], F32)
    pfg = tile("pfg", [128, 2, DC], F32)
    load("sp", modb, modb[:, :, :], modb_d[:, :, :])
    load("sp", pmg, pmg[:, :, :], pmg_d[:, :, :])
    load("sp", pfg, pfg[:, :, :], pfg_d[:, :, :])
    modc = [tile(f"modc{l}", [128, 96, 2], F32) for l in range(2)]
    sc1m = [tile(f"sc1m{l}", [128, DC, 2], F32) for l in range(2)]
    sc1f = [tile(f"sc1f{l}", [128, DC], F32) for l in range(2)]
    ggm = [tile(f"ggm{l}", [128, DC], F32) for l in range(2)]
    ggf = [tile(f"ggf{l}", [128, DC], F32) for l in range(2)]

    conv_q = []

    def conv(dst, src, res):
        conv_q.append((dst, src, res))

    def conv_pump(n):
        for _ in range(n):
            if not conv_q:
                return
            dst, src, res = conv_q.pop(0)
            S.dma("pool", dst, src, [], [res], res)

    def convert_A():
        w_r = abin_d[0].rearrange("(c p) n -> p c n", p=128)
        for s_ in range(10):
            for a in range(0, DC, 4):
                conv(wA_b[s_][:, a:a + 4, :], w_r[:, a:a + 4, s_ * 512:(s_ + 1) * 512], r_wA)

    def convert_B(l, wout_d):
        wo_r = wout_d[0].rearrange("(c p) n -> p c n", p=128)
        wg_r = wg_d[l].rearrange("(c p) n -> p c n", p=128)
        wu_r = wu_d[l].rearrange("(c p) n -> p c n", p=128)
        wdn_r = wd_d[l].rearrange("(f p) n -> p f n", p=128)
        for dc in range(DC):
            conv(wo_b[l][dc], wo_r[:, :, dc * 128:(dc + 1) * 128], r_wB[l])
        for f in range(FC):
            conv(wg_b[l][f], wg_r[:, :, f * 128:(f + 1) * 128], r_wB[l])
            conv(wu_b[l][f], wu_r[:, :, f * 128:(f + 1) * 128], r_wB[l])
        for dc in range(DC):
            for a in range(0, FC, 11):
                conv(wd_b[l][dc][:, a:a + 11, :], wdn_r[:, a:a + 11, dc * 128:(dc + 1) * 128], r_wB[l])

    def convert_C():
        w_r = cdin_d[0].rearrange("(c p) n -> p c n", p=128)
        k_ = 0
        for j in range(8):
            conv(wC_b[k_], w_r[:, :, j * 128:(j + 1) * 128], r_wC)
            conv(wC_b[k_ + 1], w_r[:, :, 1024 + j * 128:1024 + (j + 1) * 128], r_wC)
            k_ += 2
        for g in range(8):
            conv(wC_b[k_], w_r[:, :, 2048 + g * 128:2048 + (g + 1) * 128], r_wC)
            k_ += 1

    convert_A()
    conv_pump(len(conv_q))
    convert_B(0, about_d)
    convert_C()
    convert_B(1, cdout_d)

    mod_jobs = [(l, sl) for l in range(2) for sl in range(24)]
    mod_it = [0]

    def mod_job(mw, bank):
        if not mod_jobs:
            return
        l, sl = mod_jobs.pop(0)
        mwr = modw_d[l].rearrange("(c p) n -> p c n", p=128)
        w = mw[mod_it[0] % 2]
        mod_it[0] += 1
        for c4 in range(4):
            load("sp", w, w[:, c4 * 4:(c4 + 1) * 4, :], mwr[:, c4 * 4:(c4 + 1) * 4, sl * 512:(sl + 1) * 512])
        for jj in range(4):
            for c in range(DC):
                mm(bank, bank[:, 2 * jj:2 * jj + 2], w[:, c, jj * 128:(jj + 1) * 128], scv[:, c, :],
                   c == 0, c == DC - 1, [w, scv])
        tt("dve", modc[l][:, sl * 4:(sl + 1) * 4, :], bank[:, 0:8].rearrange("p (a b) -> p a b", b=2),
           modb[:, l, sl * 4:(sl + 1) * 4].unsqueeze(2).to_broadcast([128, 4, 2]), ALU.add, [bank, modb], [modc[l]])

    def mod_derive(l, first):
        if first:
            ts("dve", sc1m[l][:, :, :], modc[l][:, 16:32, :], 1.0, None, ALU.add, None, [modc[l]], [sc1m[l]])
            tt("dve", ggm[l][:, :], modc[l][:, 32:48, 0], pmg[:, l, :], ALU.mult, [modc[l], pmg], [ggm[l]])
        else:
            ts("dve", sc1f[l][:, :], modc[l][:, 64:80, 0], 1.0, None, ALU.add, None, [modc[l]], [sc1f[l]])
            tt("dve", ggf[l][:, :], modc[l][:, 80:96, 0], pfg[:, l, :], ALU.mult, [modc[l], pfg], [ggf[l]])

    with ExitStack() as st0:
        stacks.append(st0)
        mw0 = [tile(f"mw{i}", [128, DC, 512], F32) for i in range(2)]
        for _ in range(12 if "m" in KV else 48):
            mod_job(mw0, banks[_ % 2])
        mod_derive(0, True)
        S.barrier()
        stacks.pop()

    def norm_mod(xTb, TBk, sc_ap, sh_ap, hT, sqc, rt, rstd, tmpf, ssbank, rd):
        for c in range(DC):
            s = sqc[c % 2]
            act(s[:, :TBk], xTb[:, c, :TBk], AF.Square, [xTb], [s])
            mm(ssbank, ssbank[:, :TBk], ones_bf[:, :], s[:, :TBk], c == 0, c == DC - 1, [ones_bf, s])
        act(rt[:, :TBk], ssbank[:, :TBk], AF.Sqrt, [ssbank], [rt], scale=1.0 / D, bias=EPS)
        recip(rstd[:, :TBk], rt[:, :TBk], [rt], [rstd])
        for c in range(DC):
            tf = tmpf[c % 2]
            stt(tf[:, :TBk], xTb[:, c, :TBk], sc_ap(c), rstd[:, :TBk], ALU.mult, ALU.mult, [xTb, rstd] + rd, [tf])
            act(hT[:, c, :TBk], tf[:, :TBk], AF.Identity, [tf] + rd, [hT], bias=sh_ap(c))

    def gelu(zb, zap, out_ap, outres, g1, g2, n, zsb):
        act(zsb[:, :n], zap, AF.Copy, [zb], [zsb])
        act(g1[:, :n], zsb[:, :n], AF.Square, [zsb], [g1])
        ts("dve", g1[:, :n], g1[:, :n], 0.044715, 1.0, ALU.mult, ALU.add, [g1], [g1])
        tt("dve", g1[:, :n], g1[:, :n], zsb[:, :n], ALU.mult, [g1, zsb], [g1])
        act(g2[:, :n], g1[:, :n], AF.Sigmoid, [g1], [g2], scale=1.5957691216057308)
        tt("dve", out_ap, g2[:, :n], zsb[:, :n], ALU.mult, [g2, zsb], [outres])

    class Stream:
        def __init__(self, name, shape, items, nslot=2, nsplit=4, rres=()):
            self.slots = [tile(f"{name}{i}", shape, BF16) for i in range(nslot)]
            self.items = items
            self.rres = list(rres)
            self.issued = 0
            self.nsplit = nsplit

        def _issue(self):
            i = self.issued
            if i >= len(self.items):
                return
            if "n" in KV and i >= len(self.slots):
                self.issued += 1
                return
            sl = self.slots[i % len(self.slots)]
            src = self.items[i]
            n1 = sl.t.shape[1]
            step = max(1, n1 // self.nsplit)
            for a in range(0, n1, step):
                load("pool", sl, sl[:, a:a + step, :], src[:, a:a + step, :], self.rres)
            self.issued += 1

        def get(self, i):
            while self.issued <= i + len(self.slots) - 1:
                if self.issued >= len(self.items):
                    break
                self._issue()
            return self.slots[i % len(self.slots)]

    bk = [0]

    def nbank(lo=0, hi=8):
        b = banks[lo + bk[0] % (hi - lo)]
        bk[0] += 1
        return b

    def transpose_in(src_d, row0, TBk, xin, xTb):
        k = 0
        for t_ in range(TBk // 128):
            xi = xin[t_ % 2]
            for hh in range(2):
                load("sp", xi, xi[:, hh * 1024:(hh + 1) * 1024], src_d[row0 + t_ * 128:row0 + (t_ + 1) * 128, hh * 1024:(hh + 1) * 1024])
            for c4 in range(4):
                b = nbank(0, 4)
                for i in range(4):
                    c = c4 * 4 + i
                    S.op("pe", lambda e, b=b, i=i, c=c, xi=xi: e.transpose(b[:, i * 128:(i + 1) * 128], xi[:, c * 128:(c + 1) * 128], ident[:, :]),
                         [xi.res, ident.res], [b.res])
                src = b[:, :].rearrange("p (a n) -> p a n", a=4)
                dst = xTb[:, c4 * 4:(c4 + 1) * 4, t_ * 128:(t_ + 1) * 128]
                if k % 2 == 0:
                    S.op("act", lambda e, dst=dst, src=src: e.copy(out=dst, in_=src), [b.res], [xTb.res])
                else:
                    S.op("dve", lambda e, dst=dst, src=src: e.tensor_copy(out=dst, in_=src), [b.res], [xTb.res])
                k += 1

    def phaseA():
        w_r = abin_d[0].rearrange("(c p) n -> p c n", p=128)
        xin = [tile(f"A_xin{i}", [128, D], F32) for i in range(2)]
        xTb = tile("A_xTb", [128, DC, TB], F32)
        hT = tile("A_hT", [128, DC, TB], BF16)
        sqc = [tile(f"A_sq{i}", [128, TB], BF16) for i in range(2)]
        rt = tile("A_rt", [128, TB], F32)
        rstd = tile("A_rstd", [128, TB], F32)
        tmpf = [tile(f"A_tf{i}", [128, TB], F32) for i in range(2)]
        rC = tile("A_rC", [128, TB], F32)
        rS = tile("A_rS", [128, TB], F32)
        qsb = [tile(f"A_qsb{i}", [128, TB], BF16) for i in range(2)]
        qf = [tile(f"A_qf{i}", [128, TB], F32) for i in range(2)]
        zsb = tile("A_zsb", [128, 512], F32)
        t1 = [tile(f"A_t1{i}", [128, TB], F32) for i in range(2)]
        t2 = [tile(f"A_t2{i}", [128, TB], F32) for i in range(2)]
        rot = [tile(f"A_rot{i}", [128, TB], BF16) for i in range(2)]
        vsb = [tile(f"A_vsb{i}", [128, 512], BF16) for i in range(2)]
        g1 = tile("A_g1", [128, 512], F32)
        g2 = tile("A_g2", [128, 512], F32)
        vgf = tile("A_vgf", [128, 512], F32)
        vt = tile("A_vt", [128, 512], F32)
        vln = [tile(f"A_vln{i}", [128, 4, 128], BF16) for i in range(2)]
        svt = [tile(f"A_svt{i}", [128, 512], F32) for i in range(2)]
        vi = [0]
        st_ = tile("A_st", [128, 8], F32)
        uTb = tile("A_uTb", [128, 8, TB], BF16)
        sTb = tile("A_sTb", [128, 8, TB], BF16)
        vng = tile("A_vng", [128, 1024], F32)
        vnb = tile("A_vnb", [128, 1024], F32)
        bsp = tile("A_bsp", [128, 8, 128], F32)
        wsT = tile("A_wsT", [128, 8, 128], BF16)
        load("sp", vng, vng[:, :], vng_d[:, :])
        load("sp", vnb, vnb[:, :], vnb_d[:, :])
        load("sp", bsp, bsp[:, :, :], bsp_d[:, :, :])
        load("pool", wsT, wsT[:, :, :], wsT_d[:, :, :])

        sched = []
        for blk in range(NB):
            sched += [(blk, s) for s in range(10)]
        sched += [(NB, s) for s in (2, 3, 4, 5)]
        items = [wA_b[s] for (_, s) in sched]
        ws = Stream("A_w", [128, DC, 512], items, nsplit=1, rres=[r_wA])
        si = 0
        cnt = [0]
        deferred = []

        def run_deferred():
            while deferred:
                deferred.pop(0)()

        for blk in range(NB + 1):
            is_ctx = blk == NB
            TBk = CTX if is_ctx else TB
            mi = 1 if is_ctx else 0
            if is_ctx:
                transpose_in(ctx_d, 0, TBk, xin, xTb)
            else:
                transpose_in(x_d, blk * TB, TBk, xin, xTb)
                store(xTb, xT_d[blk], xTb[:, :, :], [r_xT[blk]])
                load("sp", rC, rC[:, :], ropeC_d[:, blk * TB:(blk + 1) * TB])
                load("sp", rS, rS[:, :], ropeS_d[:, blk * TB:(blk + 1) * TB])
            norm_mod(xTb, TBk, lambda c: sc1m[0][:, c, mi:mi + 1], lambda c: modc[0][:, c, mi:mi + 1],
                     hT, sqc, rt, rstd, tmpf, banks[7], [sc1m[0], modc[0]])
            slabs = (2, 3, 4, 5) if is_ctx else range(10)
            if CUT <= 2 or (is_ctx and "c" in KV):
                slabs = ()
            elif CUT < 90:
                slabs = [s for s in slabs if s < (CUT - 2) * 2]
            for s in slabs:
                W = ws.get(si)
                si += 1
                conv_pump(2)
                if s < 4:
                    is_k = s >= 2
                    for j in range(4):
                        hh = (s % 2) * 4 + j
                        b = nbank(0, 4)
                        for c in range(DC):
                            mm(b, b[:, :TBk], W[:, c, j * 128:(j + 1) * 128], hT[:, c, :TBk], c == 0, c == DC - 1, [W, hT])
                        i2 = cnt[0] % 2
                        cnt[0] += 1
                        if is_ctx:
                            act(rot[i2][:, :TBk], b[:, :TBk], AF.Copy, [b], [rot[i2]])
                            store(rot[i2], kT_d[hh].rearrange("t d n -> (t d) n")[:, Tn:Tn + CTX], rot[i2][:, :TBk], [r_kT])
                            continue
                        act(qf[i2][:, :], b[:, :], AF.Copy, [b], [qf[i2]])
                        act(qsb[i2][:, :], b[:, :], AF.Copy, [b], [qsb[i2]])
                        def rope_tail(i2=i2, hh=hh, is_k=is_k, blk=blk):
                            pb = nbank(4, 6)
                            mm(pb, pb[:, :], permT[:, :], qsb[i2][:, :], True, True, [permT, qsb[i2]])
                            tt("dve", t1[i2][:, :], qf[i2][:, :], rC[:, :], ALU.mult, [qf[i2], rC], [t1[i2]])
                            tt("dve", t2[i2][:, :], pb[:, :], rS[:, :], ALU.mult, [pb, rS], [t2[i2]])
                            tt("dve", rot[i2][:, :], t1[i2][:, :], t2[i2][:, :], ALU.add, [t1[i2], t2[i2]], [rot[i2]])
                            dst = (kT_d if is_k else qT_d)[hh].rearrange("t d n -> (t d) n")[:, blk * TB:(blk + 1) * TB]
                            store(rot[i2], dst, rot[i2][:, :], [r_kT if is_k else r_qT])
                        run_deferred()
                        deferred.append(rope_tail)
                elif s < 6:
                    row0 = Tn if is_ctx else blk * TB
                    for t_ in range(TBk // 128):
                        b = nbank(0, 4)
                        for c in range(DC):
                            mm(b, b[:, :], hT[:, c, t_ * 128:(t_ + 1) * 128], W[:, c, :], c == 0, c == DC - 1, [W, hT])
                        i2 = cnt[0] % 2
                        cnt[0] += 1
                        run_deferred()
                        act(vsb[i2][:, :], b[:, :], AF.Copy, [b], [vsb[i2]])
                        store(vsb[i2], vS_d[row0 + t_ * 128:row0 + (t_ + 1) * 128, (s - 4) * 512:(s - 3) * 512], vsb[i2][:, :], [r_vS])
                elif s < 8:
                    for j in range(4):
                        g = (s - 6) * 4 + j
                        b = nbank(0, 4)
                        for c in range(DC):
                            mm(b, b[:, :], W[:, c, j * 128:(j + 1) * 128], hT[:, c, :], c == 0, c == DC - 1, [W, hT])
                        run_deferred()
                        gelu(b, b[:, :], uTb[:, g, :], uTb, g1, g2, TB, zsb)
                else:
                    g0 = (s - 8) * 4
                    for t_ in range(4):
                        b = nbank(0, 4)
                        for c in range(DC):
                            mm(b, b[:, :], hT[:, c, t_ * 128:(t_ + 1) * 128], W[:, c, :], c == 0, c == DC - 1, [W, hT])
                        gelu(b, b[:, :], vgf[:, :], vgf, g1, g2, 512, zsb)
                        v3 = vgf[:, :].rearrange("p (g c) -> p g c", g=4)
                        S.op("dve", lambda e, v3=v3: e.tensor_reduce(out=st_[:, 0:4], in_=v3, axis=AX.X, op=ALU.add), [vgf.res], [st_.res])
                        act(vt[:, :], vgf[:, :], AF.Square, [vgf], [vt])
                        vt3 = vt[:, :].rearrange("p (g c) -> p g c", g=4)
                        S.op("dve", lambda e, vt3=vt3: e.tensor_reduce(out=st_[:, 4:8], in_=vt3, axis=AX.X, op=ALU.add), [vt.res], [st_.res])
                        ts("dve", st_[:, 0:4], st_[:, 0:4], 1.0 / 128, None, ALU.mult, None, [st_], [st_])
                        tt("dve", vt[:, 0:4], st_[:, 0:4], st_[:, 0:4], ALU.mult, [st_], [vt])
                        stt(st_[:, 4:8], st_[:, 4:8], 1.0 / 128, vt[:, 0:4], ALU.mult, ALU.subtract, [st_, vt], [st_])
                        act(st_[:, 4:8], st_[:, 4:8], AF.Sqrt, [st_], [st_], bias=EPS)
                        recip(st_[:, 4:8], st_[:, 4:8], [st_], [st_])
                        tt("dve", vt3, v3, st_[:, 0:4].unsqueeze(2).to_broadcast([128, 4, 128]), ALU.subtract, [vgf, st_], [vt])
                        tt("dve", vt3, vt3, st_[:, 4:8].unsqueeze(2).to_broadcast([128, 4, 128]), ALU.mult, [vt, st_], [vt])
                        tt("dve", vt[:, :], vt[:, :], vng[:, g0 * 128:(g0 + 4) * 128], ALU.mult, [vt, vng], [vt])
                        run_deferred()
                        vl = vln[vi[0] % 2]
                        vi[0] += 1
                        tt("dve", vl[:, :, :], vt3, vnb[:, g0 * 128:(g0 + 4) * 128].rearrange("p (g c) -> p g c", g=4), ALU.add, [vt, vnb], [vl])

                        def spatial_tail(vl=vl, g0=g0, t_=t_):
                            sb_ = nbank(4, 6)
                            for gi in range(4):
                                mm(sb_, sb_[:, gi * 128:(gi + 1) * 128], vl[:, gi, :], wsT[:, g0 + gi, :], True, True, [vl, wsT])
                            sv = svt[t_ % 2]
                            sv3 = sv[:, :].rearrange("p (g c) -> p g c", g=4)
                            tt("dve", sv3, sb_[:, :].rearrange("p (g c) -> p g c", g=4), bsp[:, g0:g0 + 4, :], ALU.add, [sb_, bsp], [sv])
                            tt("dve", sTb[:, g0:g0 + 4, t_ * 128:(t_ + 1) * 128], sv3, uTb[:, g0:g0 + 4, t_ * 128:(t_ + 1) * 128], ALU.mult, [sv, uTb], [sTb])
                        deferred.append(spatial_tail)
            run_deferred()
            if not is_ctx and CUT > 6:
                store(sTb, mixT_d[blk][:, 8:16, :], sTb[:, :, :], [r_mixT[blk]])

    def phaseATT():
        KT = [tile(f"T_KT{i}", [128, NK], BF16) for i in range(2)]
        Vh = [tile(f"T_V{i}", [128, NKT, 128], BF16) for i in range(2)]
        qt = [tile(f"T_q{i}", [128, 2, TB], BF16) for i in range(2)]
        for q__ in qt:
            S.op("dve", lambda e, q__=q__: e.memset(q__[:, :, :], 0.0), [], [q__.res])
        ET = [tile(f"T_E{i}", [128, 2 * TB], BF16) for i in range(3)]
        R1 = tile("T_R1", [128, TB], F32)
        R2 = tile("T_R2", [128, TB], F32)
        o2 = tile("T_o2", [128, TB], F32)
        osq = tile("T_osq", [128, TB], BF16)
        rt = tile("T_rt", [128, TB], F32)
        aT = [tile(f"T_aT{i}", [128, TB], BF16) for i in range(2)]
        mwT = [tile(f"T_mw{i}", [128, DC, 512], F32) for i in range(2)]
        njobs = len(mod_jobs)
        lamv = tile("T_lamv", [128, 4, 64], F32)
        lt = tile("T_lt", [128, 2, 64], F32)
        l2 = tile("T_l2", [128, 2], F32)
        lam = tile("T_lam", [128, 1], F32)
        subg = tile("T_subg", [128, 1], F32)
        load("sp", lamv, lamv[:, :, :], lamv_d[:, :, :])
        load("sp", subg, subg[:, :], subg_d[:, :])
        lam_init = 0.8 - 0.6 * float(np.exp(-0.3 * 0))
        tt("dve", lt[:, 0, :], lamv[:, 0, :], lamv[:, 1, :], ALU.mult, [lamv], [lt])
        tt("dve", lt[:, 1, :], lamv[:, 2, :], lamv[:, 3, :], ALU.mult, [lamv], [lt])
        S.op("dve", lambda e: e.tensor_reduce(out=l2[:, :], in_=lt[:, :, :], axis=AX.X, op=ALU.add), [lt.res], [l2.res])
        act(l2[:, :], l2[:, :], AF.Exp, [l2], [l2])
        tt("dve", lam[:, :], l2[:, 0:1], l2[:, 1:2], ALU.subtract, [l2], [lam])
        ts("dve", lam[:, :], lam[:, :], lam_init, None, ALU.add, None, [lam], [lam])
        ts("dve", subg[:, :], subg[:, :], 1.0 - lam_init, None, ALU.mult, None, [subg], [subg])
        bO = [banks[4], banks[5]]
        bL = [banks[6], banks[7]]
        bss = banks[7]
        NKP = NKT // 2
        pending = []
        o1s = [tile(f"T_o1{i}", [128, TB], F32) for i in range(2)]

        def load_head(h):
            load("sp", KT[h % 2], KT[h % 2][:, :], kT_d[h].rearrange("t d n -> (t d) n"), [r_kT])
            vr = vS_d.rearrange("(kt p) (h d) -> h p kt d", p=128, d=128)[h]
            half = NKT // 2
            load("sp", Vh[h % 2], Vh[h % 2][:, 0:half, :], vr[:, 0:half, :], [r_vS])
            load("sp", Vh[h % 2], Vh[h % 2][:, half:NKT, :], vr[:, half:NKT, :], [r_vS])

        load_head(0)
        ei = 0
        it = 0
        for h in range(8):
            if h + 1 < 8:
                load_head(h + 1)
            K_, V_ = KT[h % 2], Vh[h % 2]
            for qb in range(NB):
                q_ = qt[it % 2]
                load("sp", q_, q_[0:64, 0, :], qT_d[h][0][:, qb * TB:(qb + 1) * TB], [r_qT])
                load("sp", q_, q_[64:128, 1, :], qT_d[h][1][:, qb * TB:(qb + 1) * TB], [r_qT])
                steps = [(t, kp) for t in range(2) for kp in range(NKP)]

                def smm(i):
                    t, kp = steps[i]
                    for u_ in range(2):
                        b = banks[(i % 2) * 2 + u_]
                        kt = kp * 2 + u_
                        mm(b, b[:, :], K_[:, kt * 128:(kt + 1) * 128], q_[:, t, :], True, True, [K_, q_])

                smm(0)
                for i, (t, kp) in enumerate(steps):
                    if i + 1 < len(steps):
                        smm(i + 1)
                    if i == min(6, NKP - 1):
                        while pending:
                            pending.pop(0)()
                    b0, b1 = banks[(i % 2) * 2], banks[(i % 2) * 2 + 1]
                    E = ET[ei % 3]
                    ei += 1
                    act(E[:, :], dbl[i % 2][:, 0:1024], AF.Exp, [b0, b1], [E], scale=0.125)
                    for u_ in range(2):
                        kt = kp * 2 + u_
                        mm(bO[t], bO[t][:, :], V_[:, kt, :], E[:, u_ * 512:(u_ + 1) * 512], kt == 0, kt == NKT - 1, [V_, E])
                        mm(bL[t], bL[t][:, :], ones_bf[:, :], E[:, u_ * 512:(u_ + 1) * 512], kt == 0, kt == NKT - 1, [ones_bf, E])
                o1 = o1s[it % 2]
                for src_b, dst_t in ((bL[0], R1), (bL[1], R2), (bO[0], o1), (bO[1], o2)):
                    S.op("dve", lambda e, src_b=src_b, dst_t=dst_t: e.tensor_copy(out=dst_t[:, :], in_=src_b[:, :]), [src_b.res], [dst_t.res])
                recip(R1[:, :], R1[:, :], [R1], [R1])
                recip(R2[:, :], R2[:, :], [R2], [R2])
                ts("dve", R2[:, :], R2[:, :], lam[:, 0:1], None, ALU.mult, None, [R2, lam], [R2])
                tt("dve", o1[:, :], o1[:, :], R1[:, :], ALU.mult, [o1, R1], [o1])
                tt("dve", o2[:, :], o2[:, :], R2[:, :], ALU.mult, [o2, R2], [o2])
                tt("dve", o1[:, :], o1[:, :], o2[:, :], ALU.subtract, [o1, o2], [o1])

                def tail(o1=o1, a_=aT[it % 2], qb=qb, h=h):
                    act(osq[:, :], o1[:, :], AF.Square, [o1], [osq])
                    mm(bss, bss[:, :], ones_bf[:, :], osq[:, :], True, True, [ones_bf, osq])
                    act(rt[:, :], bss[:, :], AF.Sqrt, [bss], [rt], scale=1.0 / 128, bias=EPS)
                    recip(rt[:, :], rt[:, :], [rt], [rt])
                    stt(a_[:, :], o1[:, :], subg[:, 0:1], rt[:, :], ALU.mult, ALU.mult, [o1, subg, rt], [a_])
                    store(a_, mixT_d[qb][:, h, :], a_[:, :], [r_mixT[qb]])
                pending.append(tail)
                it += 1
                conv_pump(4)
                while mod_jobs and (njobs - len(mod_jobs)) * 8 * NB < it * njobs:
                    mod_job(mwT, banks[1])
        while pending:
            pending.pop(0)()
        while mod_jobs:
            mod_job(mwT, banks[1])
        mod_derive(0, False)
        mod_derive(1, True)
        mod_derive(1, False)

    def phaseB(l, wout_d, last):
        conv_pump(len(conv_q))
        wo_r = wout_d[0].rearrange("(c p) n -> p c n", p=128)
        wg_r = wg_d[l].rearrange("(c p) n -> p c n", p=128)
        wu_r = wu_d[l].rearrange("(c p) n -> p c n", p=128)
        wdn_r = wd_d[l].rearrange("(f p) n -> p f n", p=128)
        mh = tile(f"B{l}_mh", [128, DC, TB], BF16)
        xTb = tile(f"B{l}_xTb", [128, DC, TB], F32)
        yb = tile(f"B{l}_yb", [128, DC, TB], F32)
        actT = tile(f"B{l}_actT", [128, FC, TB], BF16)
        sqc = [tile(f"B{l}_sq{i}", [128, TB], BF16) for i in range(2)]
        rt = tile(f"B{l}_rt", [128, TB], F32)
        rstd = tile(f"B{l}_rstd", [128, TB], F32)
        tmpf = [tile(f"B{l}_tf{i}", [128, TB], F32) for i in range(2)]
        items_o, items_g, items_u, items_d = [], [], [], []
        for blk in range(NB):
            items_o += [wo_b[l][dc] for dc in range(DC)]
            items_g += [wg_b[l][f] for f in range(FC)]
            items_u += [wu_b[l][f] for f in range(FC)]
            for dc in range(DC):
                items_d += [wd_b[l][dc][:, 0:FC // 2, :], wd_b[l][dc][:, FC // 2:FC, :]]
        so = Stream(f"B{l}_wo", [128, DC, 128], items_o, nslot=3, nsplit=1, rres=[r_wB[l]])
        sg = Stream(f"B{l}_wg", [128, DC, 128], items_g, nslot=3, nsplit=1, rres=[r_wB[l]])
        su = Stream(f"B{l}_wu", [128, DC, 128], items_u, nslot=3, nsplit=1, rres=[r_wB[l]])
        sd = Stream(f"B{l}_wd", [128, FC // 2, 128], items_d, nslot=4, nsplit=1, rres=[r_wB[l]])
        bss = banks[7]

        def post_norm_residual(gg):
            act(rt[:, :], bss[:, :], AF.Sqrt, [bss], [rt], scale=1.0 / D, bias=EPS)
            recip(rstd[:, :], rt[:, :], [rt], [rstd])
            for dc in range(DC):
                stt(yb[:, dc, :], yb[:, dc, :], gg[:, dc:dc + 1], rstd[:, :], ALU.mult, ALU.mult, [yb, gg, rstd], [yb])
                tt("dve", xTb[:, dc, :], xTb[:, dc, :], yb[:, dc, :], ALU.add, [xTb, yb], [xTb])

        for blk in range(NB):
            load("sp", mh, mh[:, :, :], mixT_d[blk], [r_mixT[blk]])
            load("sp", xTb, xTb[:, :, :], xT_d[blk], [r_xT[blk]])
            for dc in range(DC):
                W = so.get(blk * DC + dc)
                b = nbank(0, 4)
                for c in range(DC):
                    mm(b, b[:, :], W[:, c, :], mh[:, c, :], c == 0, c == DC - 1, [W, mh])
                act(yb[:, dc, :], b[:, :], AF.Copy, [b], [yb])
                s = sqc[dc % 2]
                tt("dve", s[:, :], yb[:, dc, :], yb[:, dc, :], ALU.mult, [yb], [s])
                mm(bss, bss[:, :], ones_bf[:, :], s[:, :], dc == 0, dc == DC - 1, [ones_bf, s])
            post_norm_residual(ggm[l])
            norm_mod(xTb, TB, lambda c: sc1f[l][:, c:c + 1], lambda c: modc[l][:, 48 + c, 0:1],
                     mh, sqc, rt, rstd, tmpf, banks[6], [sc1f[l], modc[l]])
            for f in range(FC):
                Wg = sg.get(blk * FC + f)
                Wu = su.get(blk * FC + f)
                bg = nbank(0, 4)
                bu = nbank(0, 4)
                for c in range(DC):
                    mm(bg, bg[:, :], Wg[:, c, :], mh[:, c, :], c == 0, c == DC - 1, [Wg, mh])
                for c in range(DC):
                    mm(bu, bu[:, :], Wu[:, c, :], mh[:, c, :], c == 0, c == DC - 1, [Wu, mh])
                tf = tmpf[f % 2]
                act(tf[:, :], bg[:, :], AF.Silu, [bg], [tf])
                tt("dve", actT[:, f, :], tf[:, :], bu[:, :], ALU.mult, [tf, bu], [actT])
            for dc in range(DC):
                b = nbank(4, 6)
                for hf in range(2):
                    W = sd.get((blk * DC + dc) * 2 + hf)
                    for f2 in range(FC // 2):
                        f = hf * (FC // 2) + f2
                        mm(b, b[:, :], W[:, f2, :], actT[:, f, :], f == 0, f == FC - 1, [W, actT])
                act(yb[:, dc, :], b[:, :], AF.Copy, [b], [yb])
                s = sqc[dc % 2]
                tt("dve", s[:, :], yb[:, dc, :], yb[:, dc, :], ALU.mult, [yb], [s])
                mm(bss, bss[:, :], ones_bf[:, :], s[:, :], dc == 0, dc == DC - 1, [ones_bf, s])
            post_norm_residual(ggf[l])
            if not last:
                store(xTb, xT_d[blk], xTb[:, :, :], [r_xT[blk]])
            else:
                for t_ in range(4):
                    for c4 in range(4):
                        b = nbank(0, 4)
                        for i in range(4):
                            c = c4 * 4 + i
                            S.op("pe", lambda e, b=b, i=i, c=c, t_=t_: e.transpose(b[:, i * 128:(i + 1) * 128], xTb[:, c, t_ * 128:(t_ + 1) * 128], ident[:, :]),
                                 [xTb.res, ident.res], [b.res])
                        if c4 % 2 == 0:
                            act(yb[:, t_ * 4 + c4, :], b[:, :], AF.Copy, [b], [yb])
                        else:
                            S.op("dve", lambda e, b=b, c4=c4, t_=t_: e.tensor_copy(out=yb[:, t_ * 4 + c4, :], in_=b[:, :]), [b.res], [yb.res])
                    r0 = blk * TB + t_ * 128
                    store(yb, out_d[r0:r0 + 128, :].rearrange("p (a n) -> p a n", a=4), yb[:, t_ * 4:(t_ + 1) * 4, :], [r_out])

    def phaseC1():
        w_r = cdin_d[0].rearrange("(c p) n -> p c n", p=128)
        xTb = tile("C_xTb", [128, DC, TB], F32)
        hT = tile("C_hT", [128, DC, TB], BF16)
        sqc = [tile(f"C_sq{i}", [128, TB], BF16) for i in range(2)]
        rt = tile("C_rt", [128, TB], F32)
        rstd = tile("C_rstd", [128, TB], F32)
        tmpf = [tile(f"C_tf{i}", [128, TB], F32) for i in range(2)]
        sgm = [tile(f"C_sg{i}", [128, TB], F32) for i in range(2)]
        glu = [tile(f"C_glu{i}", [128, TB], F32) for i in range(2)]
        FT = [tile(f"C_FT{i}", [128, TB], BF16) for i in range(2)]
        GGb = [tile(f"C_GGb{i}", [128, 4, 256], BF16) for i in range(2)]
        ccs = tile("C_ccs", [128, 256], BF16)
        zt = tile("C_zt", [128, 16], F32)
        load("pool", ccs, ccs[:, :], ccs_d[:, :])
        S.op("dve", lambda e: e.memset(zt[:, :], 0.0), [], [zt.res])
        for j in range(8):
            store(zt, G_d[j][:, 0:16], zt[:, :], [r_G])
            store(zt, G_d[j][:, 16 + Tn:32 + Tn], zt[:, :], [r_G])
        items = []
        for blk in range(NB):
            items += [wC_b[k_] for k_ in range(24)]
        ws = Stream("C_w", [128, DC, 128], items, nslot=3, nsplit=1, rres=[r_wC])
        si = 0
        k = 0
        for blk in range(NB):
            load("sp", xTb, xTb[:, :, :], xT_d[blk], [r_xT[blk]])
            norm_mod(xTb, TB, lambda c: sc1m[1][:, c, 0:1], lambda c: modc[1][:, c, 0:1],
                     hT, sqc, rt, rstd, tmpf, banks[7], [sc1m[1], modc[1]])
            for j in range(8):
                Wa = ws.get(si)
                ba = nbank(0, 4)
                for c in range(DC):
                    mm(ba, ba[:, :], Wa[:, c, :], hT[:, c, :], c == 0, c == DC - 1, [Wa, hT])
                Wg = ws.get(si + 1)
                si += 2
                bg = nbank(0, 4)
                for c in range(DC):
                    mm(bg, bg[:, :], Wg[:, c, :], hT[:, c, :], c == 0, c == DC - 1, [Wg, hT])
                s_, g_ = sgm[k % 2], glu[k % 2]
                k += 1
                act(s_[:, :], bg[:, :], AF.Sigmoid, [bg], [s_])
                tt("dve", g_[:, :], s_[:, :], ba[:, :], ALU.mult, [s_, ba], [g_])
                store(g_, G_d[j][:, 16 + blk * TB:16 + (blk + 1) * TB], g_[:, :], [r_G])
            for g in range(8):
                Wf = ws.get(si)
                si += 1
                bf = nbank(0, 4)
                for c in range(DC):
                    mm(bf, bf[:, :], Wf[:, c, :], hT[:, c, :], c == 0, c == DC - 1, [Wf, hT])
                F_ = FT[g % 2]
                act(F_[:, :], bf[:, :], AF.Copy, [bf], [F_])
                GG = GGb[g % 2]
                for pr in range(2):
                    b2 = nbank(4, 6)
                    for tq in range(2):
                        t_ = pr * 2 + tq
                        mm(b2, b2[:, tq * 256:(tq + 1) * 256], F_[:, t_ * 128:(t_ + 1) * 128], ccs[:, :], True, True, [F_, ccs])
                    S.op("dve", lambda e, GG=GG, b2=b2, pr=pr: e.tensor_copy(out=GG[:, pr * 2:pr * 2 + 2, :], in_=b2[:, :].rearrange("p (a n) -> p a n", a=2)),
                         [b2.res], [GG.res])
                store(GG, GG_d[g][:, blk * 4:(blk + 1) * 4, :], GG[:, :, :], [r_GG])

    def phaseC2():
        yield
        tab = tile("F_tab", [128, NT, 2, TB], BF16)
        GGt = [tile(f"F_GG{i}", [128, NT, 256], BF16) for i in range(2)]
        yo = [tile(f"F_yo{i}", [128, TB], BF16) for i in range(2)]
        scale = 1.0 / float(np.sqrt(Tn * 128.0))
        it = 0
        for kb in range(NB):
            step = max(1, NT // 8)
            for a in range(0, NT, step):
                load("sp", tab, tab[:, a:a + step, :, :], dft_d[kb][:, a:a + step, :, :])
            for g in range(8):
                G_ = GGt[it % 2]
                load("sp", G_, G_[:, :, :], GG_d[g], [r_GG])
                b = nbank(0, 4)
                for nt in range(NT):
                    mm(b, b[:, :], G_[:, nt, 0:128], tab[:, nt, 0, :], nt == 0, False, [G_, tab])
                    mm(b, b[:, :], G_[:, nt, 128:256], tab[:, nt, 1, :], False, nt == NT - 1, [G_, tab])
                y_ = yo[it % 2]
                act(y_[:, :], b[:, :], AF.Copy, [b], [y_], scale=scale)
                store(y_, mixT_d[kb][:, 8 + g, :], y_[:, :], [r_mixT[kb]])
                it += 1
                yield

    def phaseC3():
        gin = [tile(f"V_gin{i}", [128, TB + 32], F32) for i in range(2)]
        yc = tile("V_yc", [128, 8, TB], F32)
        ybf = [tile(f"V_ybf{i}", [128, TB], BF16) for i in range(2)]
        ysq = [tile(f"V_ysq{i}", [128, TB], BF16) for i in range(2)]
        mu = tile("V_mu", [128, TB], F32)
        var = tile("V_var", [128, TB], F32)
        tmp = [tile(f"V_tmp{i}", [128, TB], F32) for i in range(2)]
        yo = tile("V_yo", [128, 8, TB], BF16)
        gbf = [tile(f"V_gbf{i}", [128, TB + 32], BF16) for i in range(2)]
        Dg = tile("V_Dg", [128, 8, 31, 128], BF16)
        dww = tile("V_dww", [128, 8, 31], F32)
        dwb = tile("V_dwb", [128, 8], F32)
        cng = tile("V_cng", [128, 8], F32)
        cnb = tile("V_cnb", [128, 8], F32)
        load("sp", dww, dww[:, :, :], dww_d[:, :, :])
        load("sp", dwb, dwb[:, :], dwb_d[:, :])
        load("sp", cng, cng[:, :], cng_d[:, :])
        load("sp", cnb, cnb[:, :], cnb_d[:, :])
        for j in range(8):
            for k in range(31):
                ts("dve", Dg[:, j, k, :], ident[:, :], dww[:, j, k:k + 1], None, ALU.mult, None, [ident, dww], [Dg])
        b1, b2 = banks[6], banks[7]
        it = 0
        for blk in range(NB):
            for j in range(8):
                it += 1
                g_ = gin[it % 2]
                load("sp", g_, g_[:, 0:TB + 30], G_d[j][:, 1 + blk * TB:1 + blk * TB + TB + 30], [r_G])
                gb_ = gbf[it % 2]
                act(gb_[:, 0:TB + 30], g_[:, 0:TB + 30], AF.Copy, [g_], [gb_])
                cb = nbank(4, 6)
                for k in range(31):
                    mm(cb, cb[:, :], Dg[:, j, k, :], gb_[:, k:k + TB], k == 0, k == 30, [Dg, gb_])
                yield
                act(yc[:, j, :], cb[:, :], AF.Identity, [cb, dwb], [yc], bias=dwb[:, j:j + 1])
                yb_, ys_ = ybf[j % 2], ysq[j % 2]
                act(yb_[:, :], yc[:, j, :], AF.Copy, [yc], [yb_])
                act(ys_[:, :], yc[:, j, :], AF.Square, [yc], [ys_])
                mm(b1, b1[:, :], ones_bf[:, :], yb_[:, :], j == 0, j == 7, [ones_bf, yb_])
                mm(b2, b2[:, :], ones_bf[:, :], ys_[:, :], j == 0, j == 7, [ones_bf, ys_])
            ts("dve", mu[:, :], b1[:, :], 1.0 / 1024, None, ALU.mult, None, [b1], [mu])
            tt("dve", var[:, :], mu[:, :], mu[:, :], ALU.mult, [mu], [var])
            stt(var[:, :], b2[:, :], 1.0 / 1024, var[:, :], ALU.mult, ALU.subtract, [b2, var], [var])
            act(var[:, :], var[:, :], AF.Sqrt, [var], [var], bias=EPS)
            recip(var[:, :], var[:, :], [var], [var])
            for j in range(8):
                t_ = tmp[j % 2]
                tt("dve", t_[:, :], yc[:, j, :], mu[:, :], ALU.subtract, [yc, mu], [t_])
                tt("dve", t_[:, :], t_[:, :], var[:, :], ALU.mult, [t_, var], [t_])
                act(yo[:, j, :], t_[:, :], AF.Silu, [t_, cng, cnb], [yo], scale=cng[:, j:j + 1], bias=cnb[:, j:j + 1])
            store(yo, mixT_d[blk][:, 0:8, :], yo[:, :, :], [r_mixT[blk]])

    def phaseC23():
        gens = [phaseC3(), phaseC2()]
        next(gens[1])
        while gens:
            for g_ in list(gens):
                try:
                    next(g_)
                except StopIteration:
                    gens.remove(g_)

    phases = [(phaseA,), (phaseATT,), (phaseB, 0, about_d, False), (phaseC1,), (phaseC23,), (phaseB, 1, cdout_d, True)]
    if only is not None:
        phases = [phases[i] for i in only]
    marks = [("phase0", S.q["pe"].n)]
    for ph in phases:
        with ExitStack() as st:
            stacks.append(st)
            ph[0](*ph[1:])
            S.barrier()
            stacks.pop()
        marks.append((ph[0].__name__ + str(ph[1] if len(ph) > 1 else ""), S.q["pe"].n))
    print("BUILD: sems", S.nsem, {k: q.n for k, q in S.q.items()}, "MARKS", marks)
    return nc


def _col(v, nch):
    return np.ascontiguousarray(np.asarray(v, np.float32).reshape(nch, 128).T)


def _consts(Tn):
    NB = Tn // TB
    NT = Tn // 128
    ident = np.eye(128, dtype=np.float32)
    P = np.zeros((128, 128), np.float32)
    for base in range(0, 128, 32):
        for i in range(16):
            P[base + i, base + i + 16] = 1.0
            P[base + i + 16, base + i] = 1.0
    permT = np.ascontiguousarray(P.T)
    n = np.arange(Tn)
    rows = (n // GRID_W).astype(np.float64)
    cols = (n % GRID_W).astype(np.float64)
    inv = 10000.0 ** (-np.arange(16, dtype=np.float64) / 16)
    ropeC = np.zeros((128, Tn), np.float64)
    ropeS = np.zeros((128, Tn), np.float64)
    for p in range(128):
        j = p % 64
        axis = j // 32
        f = j % 16
        second = (j % 32) >= 16
        ang = (rows if axis == 0 else cols) * inv[f]
        ropeC[p] = np.cos(ang)
        ropeS[p] = np.sin(ang) if second else -np.sin(ang)
    cc = np.arange(128, dtype=np.float64)
    B = 2 * np.pi * np.outer(cc, cc) / 128.0
    ccs = np.concatenate([np.cos(B), -np.sin(B)], axis=1).astype(np.float32)
    nn = np.arange(Tn, dtype=np.int64)
    prod = np.outer(nn, nn) % Tn
    A = 2 * np.pi * prod.astype(np.float64) / Tn
    Cn = np.cos(A).astype(np.float32)
    Sn = np.sin(A).astype(np.float32)
    dft = np.empty((NB, 128, NT, 2, TB), dtype=ml_dtypes.bfloat16)
    Cr = Cn.reshape(NT, 128, NB, TB).transpose(2, 1, 0, 3)
    Sr = Sn.reshape(NT, 128, NB, TB).transpose(2, 1, 0, 3)
    dft[:, :, :, 0, :] = Cr.astype(ml_dtypes.bfloat16)
    dft[:, :, :, 1, :] = Sr.astype(ml_dtypes.bfloat16)
    return dict(ident=ident, permT=permT, ropeC=ropeC.astype(np.float32), ropeS=ropeS.astype(np.float32),
                ccs=ccs, dft=dft)


def make_in_maps(inp, Tn, nb):
    f = lambda a: np.ascontiguousarray(np.asarray(a, np.float32))
    consts = _consts(Tn)
    bc = lambda v: np.ascontiguousarray(np.broadcast_to(np.asarray(v, np.float32), (128,) + np.asarray(v).shape))
    shared = dict(
        mod_w=f(inp["mod_w"]), ffn_w_gate=f(inp["ffn_w_gate"]), ffn_w_up=f(inp["ffn_w_up"]),
        ffn_w_down=f(inp["ffn_w_down"]), ab_w_in=f(inp["ab_w_in"]), ab_w_out=f(inp["ab_w_out"]),
        cd_w_in=f(inp["cd_w_in"]), cd_w_out=f(inp["cd_w_out"]),
        modb=np.ascontiguousarray(np.stack([_col(inp["mod_b"][l], 96) for l in range(2)], axis=1)),
        pmg=np.ascontiguousarray(np.stack([_col(inp["post_mix_g"][l], 16) for l in range(2)], axis=1)),
        pfg=np.ascontiguousarray(np.stack([_col(inp["post_ffn_g"][l], 16) for l in range(2)], axis=1)),
        lamv=bc(np.stack([inp["ab_lam_q1"][0], inp["ab_lam_k1"][0], inp["ab_lam_q2"][0], inp["ab_lam_k2"][0]])),
        subg=f(np.asarray(inp["ab_subln_g"][0]).reshape(128, 1)),
        vng=bc(inp["ab_vnorm_g"][0]), vnb=bc(inp["ab_vnorm_b"][0]),
        bsp=bc(inp["ab_b_spatial"][0]),
        wsT=f(np.transpose(np.asarray(inp["ab_w_spatial"][0]), (2, 0, 1))),
        dww=f(np.transpose(np.asarray(inp["cd_dw_w"][0]).reshape(31, 8, 128), (2, 1, 0))),
        dwb=_col(inp["cd_dw_b"][0], 8), cng=_col(inp["cd_norm_g"][0], 8), cnb=_col(inp["cd_norm_b"][0], 8),
        **consts,
    )
    maps = []
    for b in range(nb):
        m = dict(shared)
        m["x"] = f(inp["x"][b][:Tn])
        m["ctx"] = f(inp["ctx"][b])
        m["cvec"] = np.ascontiguousarray(np.stack([_col(inp["c"][b], 16), _col(inp["c_ctx"], 16)], axis=2))
        maps.append(m)
    return maps


def kernel(**inputs):
    Tn = 4096
    nb = 4
    nc = build(Tn)
    maps = make_in_maps(inputs, Tn, nb)
    zmap = {k: np.zeros_like(v) for k, v in maps[0].items()}
    in_maps = [maps[i] if i < nb else zmap for i in range(8)]
    res = run_bass_kernel_spmd(nc, in_maps, core_ids=list(range(8)))
    return np.stack([np.asarray(res.results[b]["out"], np.float32) for b in range(nb)], axis=0)
```

```python
import numpy as np
from contextlib import ExitStack
import ml_dtypes
import concourse.bass as bass
import concourse.mybir as mybir
from concourse.bass_utils import run_bass_kernel_spmd

F32, BF16 = mybir.dt.float32, mybir.dt.bfloat16
AF = mybir.ActivationFunctionType
ALU = mybir.AluOpType
AX = mybir.AxisListType

D = 2048
DC = 16
DFF = 5632
FC = 44
CTX = 256
EPS = 1e-6
GRID_W = 64
EPOCH = 30000
TB = 512
import os
CUT = int(os.environ.get('KCUT', '99'))
KV = os.environ.get('KV', '')


class Res:
    __slots__ = ("name", "w", "r", "dsem", "dcnt")

    def __init__(self, name):
        self.name = name
        self.w = None
        self.r = {}
        self.dsem = None
        self.dcnt = 0


class Queue:
    def __init__(self, name, h):
        self.name = name
        self.h = h
        self.n = 0
        self.sems = []
        self.known = {}


class Sched:
    def __init__(self, nc):
        self.nc = nc
        self.q = {
            "pe": Queue("pe", nc.tensor),
            "act": Queue("act", nc.scalar),
            "dve": Queue("dve", nc.vector),
            "pool": Queue("pool", nc.gpsimd),
            "sp": Queue("sp", nc.sync),
        }
        self.dres = []
        self.nsem = 0

    def _sem(self, name):
        self.nsem += 1
        return self.nc.alloc_semaphore(name)

    def _qsem(self, q, idx):
        e = idx // EPOCH
        while len(q.sems) <= e:
            q.sems.append(self._sem(f"q{q.name}{len(q.sems)}"))
        return q.sems[e], (idx % EPOCH) + 1, e

    def _wait(self, q, ev):
        if ev[0] == "q":
            _, qn, idx = ev
            if qn == q.name and qn in ("pe", "sp"):
                return
            src = self.q[qn]
            sem, val, e = self._qsem(src, idx)
            for (kq, ke), kv in q.known.items():
                if kq == qn and (ke > e or (ke == e and kv >= val)):
                    return
            q.h.wait_ge(sem, val)
            q.known[(qn, e)] = val
        else:
            res = ev[1]
            key = ("d", id(res))
            if q.known.get((key, 0), 0) >= res.dcnt:
                return
            q.h.wait_ge(res.dsem, res.dcnt)
            q.known[(key, 0)] = res.dcnt

    def _deps(self, q, reads, writes):
        for r in reads:
            if r.w is not None:
                self._wait(q, r.w)
        for w in writes:
            if w.w is not None:
                self._wait(q, w.w)
            for ev in list(w.r.values()):
                self._wait(q, ev)

    def op(self, qn, fn, reads=(), writes=()):
        q = self.q[qn]
        self._deps(q, reads, writes)
        sem, val, _ = self._qsem(q, q.n)
        ins = fn(q.h)
        ins.then_inc(sem, 1)
        me = ("q", qn, q.n)
        q.n += 1
        for r in reads:
            r.r[qn] = me
        for w in writes:
            w.w = me
            w.r = {}

    def dma(self, qn, out, in_, reads, writes, tag):
        q = self.q[qn]
        self._deps(q, reads, writes)
        if tag.dsem is None:
            tag.dsem = self._sem("d" + tag.name)
            self.dres.append(tag)
        ins = q.h.dma_start(out=out, in_=in_)
        ins.then_inc(tag.dsem, 16)
        tag.dcnt += 16
        me = ("d", tag)
        for r in reads:
            r.r[("d", id(tag))] = me
        for w in writes:
            w.w = me
            w.r = {}

    def barrier(self):
        for q in self.q.values():
            for o in self.q.values():
                if o is not q and o.n > 0:
                    self._wait(q, ("q", o.name, o.n - 1))
            for r in self.dres:
                self._wait(q, ("d", r))


class T:
    def __init__(self, h, name):
        self.t = h
        self.res = Res(name)
        self.name = name

    def __getitem__(self, k):
        return self.t[k]


def build(Tn, dbg=False, only=None):
    NB = Tn // TB
    NT = Tn // 128
    NK = Tn + CTX
    NKT = NK // 128
    nc = bass.Bass("TRN2", target_bir_lowering=False)
    S = Sched(nc)

    def din(name, shape, dt=F32):
        return nc.dram_tensor(name, list(shape), dt, kind="ExternalInput").ap()

    def dscr(name, shape, dt):
        kind = "ExternalOutput" if dbg else "Internal"
        return nc.dram_tensor(name, list(shape), dt, kind=kind).ap()

    x_d = din("x", [Tn, D])
    ctx_d = din("ctx", [CTX, D])
    cvec_d = din("cvec", [128, DC, 2])
    modw_d = din("mod_w", [2, D, 6 * D])
    modb_d = din("modb", [128, 2, 96])
    pmg_d = din("pmg", [128, 2, DC])
    pfg_d = din("pfg", [128, 2, DC])
    wg_d = din("ffn_w_gate", [2, D, DFF])
    wu_d = din("ffn_w_up", [2, D, DFF])
    wd_d = din("ffn_w_down", [2, DFF, D])
    abin_d = din("ab_w_in", [1, D, 5120])
    about_d = din("ab_w_out", [1, D, D])
    cdin_d = din("cd_w_in", [1, D, 3072])
    cdout_d = din("cd_w_out", [1, D, D])
    lamv_d = din("lamv", [128, 4, 64])
    subg_d = din("subg", [128, 1])
    vng_d = din("vng", [128, 1024])
    vnb_d = din("vnb", [128, 1024])
    bsp_d = din("bsp", [128, 8, 128])
    wsT_d = din("wsT", [128, 8, 128])
    dww_d = din("dww", [128, 8, 31])
    dwb_d = din("dwb", [128, 8])
    cng_d = din("cng", [128, 8])
    cnb_d = din("cnb", [128, 8])
    ident_d = din("ident", [128, 128])
    permT_d = din("permT", [128, 128])
    ropeC_d = din("ropeC", [128, Tn])
    ropeS_d = din("ropeS", [128, Tn])
    ccs_d = din("ccs", [128, 256])
    dft_d = din("dft", [NB, 128, NT, 2, TB], BF16)
    out_d = nc.dram_tensor("out", [Tn, D], F32, kind="ExternalOutput").ap()

    xT_d = dscr("xT_s", [NB, 128, DC, TB], F32)
    qT_d = dscr("qT_s", [8, 2, 64, Tn], BF16)
    kT_d = dscr("kT_s", [8, 2, 64, NK], BF16)
    vS_d = dscr("vS_s", [NK, 1024], BF16)
    mixT_d = dscr("mixT_s", [NB, 128, DC, TB], BF16)
    G_d = dscr("G_s", [8, 128, Tn + 32], F32)
    GG_d = dscr("GG_s", [8, 128, NT, 256], BF16)
    wA_b = nc.dram_tensor("wA_b", [10, 128, DC, 512], BF16, kind="Internal").ap()
    wC_b = nc.dram_tensor("wC_b", [24, 128, DC, 128], BF16, kind="Internal").ap()
    wo_b = [nc.dram_tensor(f"wo_b{l}", [DC, 128, DC, 128], BF16, kind="Internal").ap() for l in range(2)]
    wg_b = [nc.dram_tensor(f"wg_b{l}", [FC, 128, DC, 128], BF16, kind="Internal").ap() for l in range(2)]
    wu_b = [nc.dram_tensor(f"wu_b{l}", [FC, 128, DC, 128], BF16, kind="Internal").ap() for l in range(2)]
    wd_b = [nc.dram_tensor(f"wd_b{l}", [DC, 128, FC, 128], BF16, kind="Internal").ap() for l in range(2)]
    r_wA, r_wC = Res("wA"), Res("wC")
    r_wB = [Res("wB0"), Res("wB1")]
    r_xT = [Res(f"xT{b}") for b in range(NB)]
    r_mixT = [Res(f"mixT{b}") for b in range(NB)]
    r_qT, r_kT, r_vS, r_G, r_GG = Res("qT"), Res("kT"), Res("vS"), Res("G"), Res("GG")
    r_out = Res("out")

    stacks = [ExitStack()]

    def tile(name, shape, dt, psum=False):
        if psum:
            return T(nc.alloc_psum_tensor("ps_" + name, list(shape), dt), name)
        return T(stacks[-1].enter_context(nc.sbuf_tensor("sb_" + name, list(shape), dt)), name)

    class BankView:
        def __init__(self, base, off, name):
            self.base, self.off, self.name = base, off, name
            self.res = Res(name)

        def __getitem__(self, k):
            p, c = k
            a = 0 if c.start is None else c.start
            b = 512 if c.stop is None else c.stop
            return self.base[p, self.off + a:self.off + b]

    dbl = [nc.alloc_psum_tensor(f"ps_dbl{i}", [128, 1024], F32) for i in range(4)]
    banks = [BankView(dbl[i // 2], (i % 2) * 512, f"bank{i}") for i in range(8)]

    def mm(bank, out, lhsT, rhs, start, stop, reads):
        S.op("pe", lambda e: e.matmul(out, lhsT=lhsT, rhs=rhs, start=start, stop=stop), [x.res for x in reads], [bank.res])

    def act(out, in_, func, reads, writes, scale=1.0, bias=0.0):
        S.op("act", lambda e: e.activation(out=out, in_=in_, func=func, bias=bias, scale=scale),
             [x.res for x in reads], [x.res for x in writes])

    def tt(eng, out, in0, in1, op, reads, writes):
        S.op(eng, lambda e: e.tensor_tensor(out=out, in0=in0, in1=in1, op=op),
             [x.res for x in reads], [x.res for x in writes])

    def ts(eng, out, in0, s1, s2, op0, op1, reads, writes):
        if op1 is None:
            S.op(eng, lambda e: e.tensor_scalar(out=out, in0=in0, scalar1=s1, scalar2=None, op0=op0),
                 [x.res for x in reads], [x.res for x in writes])
        else:
            S.op(eng, lambda e: e.tensor_scalar(out=out, in0=in0, scalar1=s1, scalar2=s2, op0=op0, op1=op1),
                 [x.res for x in reads], [x.res for x in writes])

    def stt(out, in0, scalar, in1, op0, op1, reads, writes):
        S.op("dve", lambda e: e.scalar_tensor_tensor(out=out, in0=in0, scalar=scalar, in1=in1, op0=op0, op1=op1),
             [x.res for x in reads], [x.res for x in writes])

    def recip(out, in_, reads, writes):
        S.op("dve", lambda e: e.reciprocal(out=out, in_=in_), [x.res for x in reads], [x.res for x in writes])

    def load(qn, dst, out_ap, in_ap, rres=()):
        S.dma(qn, out_ap, in_ap, list(rres), [dst.res], dst.res)

    def store(src, out_ap, in_ap, wres):
        S.dma("sp", out_ap, in_ap, [src.res], list(wres), src.res)

    ones_bf = tile("ones_bf", [128, 128], BF16)
    ident = tile("ident", [128, 128], F32)
    permT = tile("permT", [128, 128], BF16)
    S.op("dve", lambda e: e.memset(ones_bf[:, :], 1.0), [], [ones_bf.res])
    load("sp", ident, ident[:, :], ident_d[:, :])
    load("pool", permT, permT[:, :], permT_d[:, :])
    cvec = tile("cvec", [128, DC, 2], F32)
    scv = tile("scv", [128, DC, 2], F32)
    load("sp", cvec, cvec[:, :, :], cvec_d[:, :, :])
    act(scv[:, :, :], cvec[:, :, :], AF.Silu, [cvec], [scv])
    modb = tile("modb", [128, 2, 96], F32)
    pmg = tile("pmg", [128, 2, DC], F32)
    pfg = tile("pfg", [128, 2, DC], F32)
    load("sp", modb, modb[:, :, :], modb_d[:, :, :])
    load("sp", pmg, pmg[:, :, :], pmg_d[:, :, :])
    load("sp", pfg, pfg[:, :, :], pfg_d[:, :, :])
    modc = [tile(f"modc{l}", [128, 96, 2], F32) for l in range(2)]
    sc1m = [tile(f"sc1m{l}", [128, DC, 2], F32) for l in range(2)]
    sc1f = [tile(f"sc1f{l}", [128, DC], F32) for l in range(2)]
    ggm = [tile(f"ggm{l}", [128, DC], F32) for l in range(2)]
    ggf = [tile(f"ggf{l}", [128, DC], F32) for l in range(2)]

    conv_q = []

    def conv(dst, src, res):
        conv_q.append((dst, src, res))

    def conv_pump(n):
        for _ in range(n):
            if not conv_q:
                return
            dst, src, res = conv_q.pop(0)
            S.dma("pool", dst, src, [], [res], res)

    def convert_A():
        w_r = abin_d[0].rearrange("(c p) n -> p c n", p=128)
        for s_ in range(10):
            for a in range(0, DC, 4):
                conv(wA_b[s_][:, a:a + 4, :], w_r[:, a:a + 4, s_ * 512:(s_ + 1) * 512], r_wA)

    def convert_B(l, wout_d):
        wo_r = wout_d[0].rearrange("(c p) n -> p c n", p=128)
        wg_r = wg_d[l].rearrange("(c p) n -> p c n", p=128)
        wu_r = wu_d[l].rearrange("(c p) n -> p c n", p=128)
        wdn_r = wd_d[l].rearrange("(f p) n -> p f n", p=128)
        for dc in range(DC):
            conv(wo_b[l][dc], wo_r[:, :, dc * 128:(dc + 1) * 128], r_wB[l])
        for f in range(FC):
            conv(wg_b[l][f], wg_r[:, :, f * 128:(f + 1) * 128], r_wB[l])
            conv(wu_b[l][f], wu_r[:, :, f * 128:(f + 1) * 128], r_wB[l])
        for dc in range(DC):
            for a in range(0, FC, 11):
                conv(wd_b[l][dc][:, a:a + 11, :], wdn_r[:, a:a + 11, dc * 128:(dc + 1) * 128], r_wB[l])

    def convert_C():
        w_r = cdin_d[0].rearrange("(c p) n -> p c n", p=128)
        k_ = 0
        for j in range(8):
            conv(wC_b[k_], w_r[:, :, j * 128:(j + 1) * 128], r_wC)
            conv(wC_b[k_ + 1], w_r[:, :, 1024 + j * 128:1024 + (j + 1) * 128], r_wC)
            k_ += 2
        for g in range(8):
            conv(wC_b[k_], w_r[:, :, 2048 + g * 128:2048 + (g + 1) * 128], r_wC)
            k_ += 1

    convert_A()
    conv_pump(len(conv_q))
    convert_B(0, about_d)
    convert_C()
    convert_B(1, cdout_d)

    mod_jobs = [(l, sl) for l in range(2) for sl in range(24)]
    mod_it = [0]

    def mod_job(mw, bank):
        if not mod_jobs:
            return
        l, sl = mod_jobs.pop(0)
        mwr = modw_d[l].rearrange("(c p) n -> p c n", p=128)
        w = mw[mod_it[0] % 2]
        mod_it[0] += 1
        for c4 in range(4):
            load("sp", w, w[:, c4 * 4:(c4 + 1) * 4, :], mwr[:, c4 * 4:(c4 + 1) * 4, sl * 512:(sl + 1) * 512])
        for jj in range(4):
            for c in range(DC):
                mm(bank, bank[:, 2 * jj:2 * jj + 2], w[:, c, jj * 128:(jj + 1) * 128], scv[:, c, :],
                   c == 0, c == DC - 1, [w, scv])
        tt("dve", modc[l][:, sl * 4:(sl + 1) * 4, :], bank[:, 0:8].rearrange("p (a b) -> p a b", b=2),
           modb[:, l, sl * 4:(sl + 1) * 4].unsqueeze(2).to_broadcast([128, 4, 2]), ALU.add, [bank, modb], [modc[l]])

    def mod_derive(l, first):
        if first:
            ts("dve", sc1m[l][:, :, :], modc[l][:, 16:32, :], 1.0, None, ALU.add, None, [modc[l]], [sc1m[l]])
            tt("dve", ggm[l][:, :], modc[l][:, 32:48, 0], pmg[:, l, :], ALU.mult, [modc[l], pmg], [ggm[l]])
        else:
            ts("dve", sc1f[l][:, :], modc[l][:, 64:80, 0], 1.0, None, ALU.add, None, [modc[l]], [sc1f[l]])
            tt("dve", ggf[l][:, :], modc[l][:, 80:96, 0], pfg[:, l, :], ALU.mult, [modc[l], pfg], [ggf[l]])

    with ExitStack() as st0:
        stacks.append(st0)
        mw0 = [tile(f"mw{i}", [128, DC, 512], F32) for i in range(2)]
        for _ in range(12 if "m" in KV else 48):
            mod_job(mw0, banks[_ % 2])
        mod_derive(0, True)
        S.barrier()
        stacks.pop()

    def norm_mod(xTb, TBk, sc_ap, sh_ap, hT, sqc, rt, rstd, tmpf, ssbank, rd):
        for c in range(DC):
            s = sqc[c % 2]
            act(s[:, :TBk], xTb[:, c, :TBk], AF.Square, [xTb], [s])
            mm(ssbank, ssbank[:, :TBk], ones_bf[:, :], s[:, :TBk], c == 0, c == DC - 1, [ones_bf, s])
        act(rt[:, :TBk], ssbank[:, :TBk], AF.Sqrt, [ssbank], [rt], scale=1.0 / D, bias=EPS)
        recip(rstd[:, :TBk], rt[:, :TBk], [rt], [rstd])
        for c in range(DC):
            tf = tmpf[c % 2]
            stt(tf[:, :TBk], xTb[:, c, :TBk], sc_ap(c), rstd[:, :TBk], ALU.mult, ALU.mult, [xTb, rstd] + rd, [tf])
            act(hT[:, c, :TBk], tf[:, :TBk], AF.Identity, [tf] + rd, [hT], bias=sh_ap(c))

    def gelu(zb, zap, out_ap, outres, g1, g2, n, zsb):
        act(zsb[:, :n], zap, AF.Copy, [zb], [zsb])
        act(g1[:, :n], zsb[:, :n], AF.Square, [zsb], [g1])
        ts("dve", g1[:, :n], g1[:, :n], 0.044715, 1.0, ALU.mult, ALU.add, [g1], [g1])
        tt("dve", g1[:, :n], g1[:, :n], zsb[:, :n], ALU.mult, [g1, zsb], [g1])
        act(g2[:, :n], g1[:, :n], AF.Sigmoid, [g1], [g2], scale=1.5957691216057308)
        tt("dve", out_ap, g2[:, :n], zsb[:, :n], ALU.mult, [g2, zsb], [outres])

    class Stream:
        def __init__(self, name, shape, items, nslot=2, nsplit=4, rres=()):
            self.slots = [tile(f"{name}{i}", shape, BF16) for i in range(nslot)]
            self.items = items
            self.rres = list(rres)
            self.issued = 0
            self.nsplit = nsplit

        def _issue(self):
            i = self.issued
            if i >= len(self.items):
                return
            if "n" in KV and i >= len(self.slots):
                self.issued += 1
                return
            sl = self.slots[i % len(self.slots)]
            src = self.items[i]
            n1 = sl.t.shape[1]
            step = max(1, n1 // self.nsplit)
            for a in range(0, n1, step):
                load("pool", sl, sl[:, a:a + step, :], src[:, a:a + step, :], self.rres)
            self.issued += 1

        def get(self, i):
            while self.issued <= i + len(self.slots) - 1:
                if self.issued >= len(self.items):
                    break
                self._issue()
            return self.slots[i % len(self.slots)]

    bk = [0]

    def nbank(lo=0, hi=8):
        b = banks[lo + bk[0] % (hi - lo)]
        bk[0] += 1
        return b

    def transpose_in(src_d, row0, TBk, xin, xTb):
        k = 0
        for t_ in range(TBk // 128):
            xi = xin[t_ % 2]
            for hh in range(2):
                load("sp", xi, xi[:, hh * 1024:(hh + 1) * 1024], src_d[row0 + t_ * 128:row0 + (t_ + 1) * 128, hh * 1024:(hh + 1) * 1024])
            for c4 in range(4):
                b = nbank(0, 4)
                for i in range(4):
                    c = c4 * 4 + i
                    S.op("pe", lambda e, b=b, i=i, c=c, xi=xi: e.transpose(b[:, i * 128:(i + 1) * 128], xi[:, c * 128:(c + 1) * 128], ident[:, :]),
                         [xi.res, ident.res], [b.res])
                src = b[:, :].rearrange("p (a n) -> p a n", a=4)
                dst = xTb[:, c4 * 4:(c4 + 1) * 4, t_ * 128:(t_ + 1) * 128]
                if k % 2 == 0:
                    S.op("act", lambda e, dst=dst, src=src: e.copy(out=dst, in_=src), [b.res], [xTb.res])
                else:
                    S.op("dve", lambda e, dst=dst, src=src: e.tensor_copy(out=dst, in_=src), [b.res], [xTb.res])
                k += 1

    def phaseA():
        w_r = abin_d[0].rearrange("(c p) n -> p c n", p=128)
        xin = [tile(f"A_xin{i}", [128, D], F32) for i in range(2)]
        xTb = tile("A_xTb", [128, DC, TB], F32)
        hT = tile("A_hT", [128, DC, TB], BF16)
        sqc = [tile(f"A_sq{i}", [128, TB], BF16) for i in range(2)]
        rt = tile("A_rt", [128, TB], F32)
        rstd = tile("A_rstd", [128, TB], F32)
        tmpf = [tile(f"A_tf{i}", [128, TB], F32) for i in range(2)]
        rC = tile("A_rC", [128, TB], F32)
        rS = tile("A_rS", [128, TB], F32)
        qsb = [tile(f"A_qsb{i}", [128, TB], BF16) for i in range(2)]
        qf = [tile(f"A_qf{i}", [128, TB], F32) for i in range(2)]
        zsb = tile("A_zsb", [128, 512], F32)
        t1 = [tile(f"A_t1{i}", [128, TB], F32) for i in range(2)]
        t2 = [tile(f"A_t2{i}", [128, TB], F32) for i in range(2)]
        rot = [tile(f"A_rot{i}", [128, TB], BF16) for i in range(2)]
        vsb = [tile(f"A_vsb{i}", [128, 512], BF16) for i in range(2)]
        g1 = tile("A_g1", [128, 512], F32)
        g2 = tile("A_g2", [128, 512], F32)
        vgf = tile("A_vgf", [128, 512], F32)
        vt = tile("A_vt", [128, 512], F32)
        vln = [tile(f"A_vln{i}", [128, 4, 128], BF16) for i in range(2)]
        svt = [tile(f"A_svt{i}", [128, 512], F32) for i in range(2)]
        vi = [0]
        st_ = tile("A_st", [128, 8], F32)
        uTb = tile("A_uTb", [128, 8, TB], BF16)
        sTb = tile("A_sTb", [128, 8, TB], BF16)
        vng = tile("A_vng", [128, 1024], F32)
        vnb = tile("A_vnb", [128, 1024], F32)
        bsp = tile("A_bsp", [128, 8, 128], F32)
        wsT = tile("A_wsT", [128, 8, 128], BF16)
        load("sp", vng, vng[:, :], vng_d[:, :])
        load("sp", vnb, vnb[:, :], vnb_d[:, :])
        load("sp", bsp, bsp[:, :, :], bsp_d[:, :, :])
        load("pool", wsT, wsT[:, :, :], wsT_d[:, :, :])

        sched = []
        for blk in range(NB):
            sched += [(blk, s) for s in range(10)]
        sched += [(NB, s) for s in (2, 3, 4, 5)]
        items = [wA_b[s] for (_, s) in sched]
        ws = Stream("A_w", [128, DC, 512], items, nsplit=1, rres=[r_wA])
        si = 0
        cnt = [0]
        deferred = []

        def run_deferred():
            while deferred:
                deferred.pop(0)()

        for blk in range(NB + 1):
            is_ctx = blk == NB
            TBk = CTX if is_ctx else TB
            mi = 1 if is_ctx else 0
            if is_ctx:
                transpose_in(ctx_d, 0, TBk, xin, xTb)
            else:
                transpose_in(x_d, blk * TB, TBk, xin, xTb)
                store(xTb, xT_d[blk], xTb[:, :, :], [r_xT[blk]])
                load("sp", rC, rC[:, :], ropeC_d[:, blk * TB:(blk + 1) * TB])
                load("sp", rS, rS[:, :], ropeS_d[:, blk * TB:(blk + 1) * TB])
            norm_mod(xTb, TBk, lambda c: sc1m[0][:, c, mi:mi + 1], lambda c: modc[0][:, c, mi:mi + 1],
                     hT, sqc, rt, rstd, tmpf, banks[7], [sc1m[0], modc[0]])
            slabs = (2, 3, 4, 5) if is_ctx else range(10)
            if CUT <= 2 or (is_ctx and "c" in KV):
                slabs = ()
            elif CUT < 90:
                slabs = [s for s in slabs if s < (CUT - 2) * 2]
            for s in slabs:
                W = ws.get(si)
                si += 1
                conv_pump(2)
                if s < 4:
                    is_k = s >= 2
                    for j in range(4):
                        hh = (s % 2) * 4 + j
                        b = nbank(0, 4)
                        for c in range(DC):
                            mm(b, b[:, :TBk], W[:, c, j * 128:(j + 1) * 128], hT[:, c, :TBk], c == 0, c == DC - 1, [W, hT])
                        i2 = cnt[0] % 2
                        cnt[0] += 1
                        if is_ctx:
                            act(rot[i2][:, :TBk], b[:, :TBk], AF.Copy, [b], [rot[i2]])
                            store(rot[i2], kT_d[hh].rearrange("t d n -> (t d) n")[:, Tn:Tn + CTX], rot[i2][:, :TBk], [r_kT])
                            continue
                        act(qf[i2][:, :], b[:, :], AF.Copy, [b], [qf[i2]])
                        act(qsb[i2][:, :], b[:, :], AF.Copy, [b], [qsb[i2]])
                        def rope_tail(i2=i2, hh=hh, is_k=is_k, blk=blk):
                            pb = nbank(4, 6)
                            mm(pb, pb[:, :], permT[:, :], qsb[i2][:, :], True, True, [permT, qsb[i2]])
                            tt("dve", t1[i2][:, :], qf[i2][:, :], rC[:, :], ALU.mult, [qf[i2], rC], [t1[i2]])
                            tt("dve", t2[i2][:, :], pb[:, :], rS[:, :], ALU.mult, [pb, rS], [t2[i2]])
                            tt("dve", rot[i2][:, :], t1[i2][:, :], t2[i2][:, :], ALU.add, [t1[i2], t2[i2]], [rot[i2]])
                            dst = (kT_d if is_k else qT_d)[hh].rearrange("t d n -> (t d) n")[:, blk * TB:(blk + 1) * TB]
                            store(rot[i2], dst, rot[i2][:, :], [r_kT if is_k else r_qT])
                        run_deferred()
                        deferred.append(rope_tail)
                elif s < 6:
                    row0 = Tn if is_ctx else blk * TB
                    for t_ in range(TBk // 128):
                        b = nbank(0, 4)
                        for c in range(DC):
                            mm(b, b[:, :], hT[:, c, t_ * 128:(t_ + 1) * 128], W[:, c, :], c == 0, c == DC - 1, [W, hT])
                        i2 = cnt[0] % 2
                        cnt[0] += 1
                        run_deferred()
                        act(vsb[i2][:, :], b[:, :], AF.Copy, [b], [vsb[i2]])
                        store(vsb[i2], vS_d[row0 + t_ * 128:row0 + (t_ + 1) * 128, (s - 4) * 512:(s - 3) * 512], vsb[i2][:, :], [r_vS])
                elif s < 8:
                    for j in range(4):
                        g = (s - 6) * 4 + j
                        b = nbank(0, 4)
                        for c in range(DC):
                            mm(b, b[:, :], W[:, c, j * 128:(j + 1) * 128], hT[:, c, :], c == 0, c == DC - 1, [W, hT])
                        run_deferred()
                        gelu(b, b[:, :], uTb[:, g, :], uTb, g1, g2, TB, zsb)
                else:
                    g0 = (s - 8) * 4
                    for t_ in range(4):
                        b = nbank(0, 4)
                        for c in range(DC):
                            mm(b, b[:, :], hT[:, c, t_ * 128:(t_ + 1) * 128], W[:, c, :], c == 0, c == DC - 1, [W, hT])
                        gelu(b, b[:, :], vgf[:, :], vgf, g1, g2, 512, zsb)
                        v3 = vgf[:, :].rearrange("p (g c) -> p g c", g=4)
                        S.op("dve", lambda e, v3=v3: e.tensor_reduce(out=st_[:, 0:4], in_=v3, axis=AX.X, op=ALU.add), [vgf.res], [st_.res])
                        act(vt[:, :], vgf[:, :], AF.Square, [vgf], [vt])
                        vt3 = vt[:, :].rearrange("p (g c) -> p g c", g=4)
                        S.op("dve", lambda e, vt3=vt3: e.tensor_reduce(out=st_[:, 4:8], in_=vt3, axis=AX.X, op=ALU.add), [vt.res], [st_.res])
                        ts("dve", st_[:, 0:4], st_[:, 0:4], 1.0 / 128, None, ALU.mult, None, [st_], [st_])
                        tt("dve", vt[:, 0:4], st_[:, 0:4], st_[:, 0:4], ALU.mult, [st_], [vt])
                        stt(st_[:, 4:8], st_[:, 4:8], 1.0 / 128, vt[:, 0:4], ALU.mult, ALU.subtract, [st_, vt], [st_])
                        act(st_[:, 4:8], st_[:, 4:8], AF.Sqrt, [st_], [st_], bias=EPS)
                        recip(st_[:, 4:8], st_[:, 4:8], [st_], [st_])
                        tt("dve", vt3, v3, st_[:, 0:4].unsqueeze(2).to_broadcast([128, 4, 128]), ALU.subtract, [vgf, st_], [vt])
                        tt("dve", vt3, vt3, st_[:, 4:8].unsqueeze(2).to_broadcast([128, 4, 128]), ALU.mult, [vt, st_], [vt])
                        tt("dve", vt[:, :], vt[:, :], vng[:, g0 * 128:(g0 + 4) * 128], ALU.mult, [vt, vng], [vt])
                        run_deferred()
                        vl = vln[vi[0] % 2]
                        vi[0] += 1
                        tt("dve", vl[:, :, :], vt3, vnb[:, g0 * 128:(g0 + 4) * 128].rearrange("p (g c) -> p g c", g=4), ALU.add, [vt, vnb], [vl])

                        def spatial_tail(vl=vl, g0=g0, t_=t_):
                            sb_ = nbank(4, 6)
                            for gi in range(4):
                                mm(sb_, sb_[:, gi * 128:(gi + 1) * 128], vl[:, gi, :], wsT[:, g0 + gi, :], True, True, [vl, wsT])
                            sv = svt[t_ % 2]
                            sv3 = sv[:, :].rearrange("p (g c) -> p g c", g=4)
                            tt("dve", sv3, sb_[:, :].rearrange("p (g c) -> p g c", g=4), bsp[:, g0:g0 + 4, :], ALU.add, [sb_, bsp], [sv])
                            tt("dve", sTb[:, g0:g0 + 4, t_ * 128:(t_ + 1) * 128], sv3, uTb[:, g0:g0 + 4, t_ * 128:(t_ + 1) * 128], ALU.mult, [sv, uTb], [sTb])
                        deferred.append(spatial_tail)
            run_deferred()
            if not is_ctx and CUT > 6:
                store(sTb, mixT_d[blk][:, 8:16, :], sTb[:, :, :], [r_mixT[blk]])

    def phaseATT():
        KT = [tile(f"T_KT{i}", [128, NK], BF16) for i in range(2)]
        Vh = [tile(f"T_V{i}", [128, NKT, 128], BF16) for i in range(2)]
        qt = [tile(f"T_q{i}", [128, 2, TB], BF16) for i in range(2)]
        for q__ in qt:
            S.op("dve", lambda e, q__=q__: e.memset(q__[:, :, :], 0.0), [], [q__.res])
        ET = [tile(f"T_E{i}", [128, 2 * TB], BF16) for i in range(3)]
        R1 = tile("T_R1", [128, TB], F32)
        R2 = tile("T_R2", [128, TB], F32)
        o2 = tile("T_o2", [128, TB], F32)
        osq = tile("T_osq", [128, TB], BF16)
        rt = tile("T_rt", [128, TB], F32)
        aT = [tile(f"T_aT{i}", [128, TB], BF16) for i in range(2)]
        mwT = [tile(f"T_mw{i}", [128, DC, 512], F32) for i in range(2)]
        njobs = len(mod_jobs)
        lamv = tile("T_lamv", [128, 4, 64], F32)
        lt = tile("T_lt", [128, 2, 64], F32)
        l2 = tile("T_l2", [128, 2], F32)
        lam = tile("T_lam", [128, 1], F32)
        subg = tile("T_subg", [128, 1], F32)
        load("sp", lamv, lamv[:, :, :], lamv_d[:, :, :])
        load("sp", subg, subg[:, :], subg_d[:, :])
        lam_init = 0.8 - 0.6 * float(np.exp(-0.3 * 0))
        tt("dve", lt[:, 0, :], lamv[:, 0, :], lamv[:, 1, :], ALU.mult, [lamv], [lt])
        tt("dve", lt[:, 1, :], lamv[:, 2, :], lamv[:, 3, :], ALU.mult, [lamv], [lt])
        S.op("dve", lambda e: e.tensor_reduce(out=l2[:, :], in_=lt[:, :, :], axis=AX.X, op=ALU.add), [lt.res], [l2.res])
        act(l2[:, :], l2[:, :], AF.Exp, [l2], [l2])
        tt("dve", lam[:, :], l2[:, 0:1], l2[:, 1:2], ALU.subtract, [l2], [lam])
        ts("dve", lam[:, :], lam[:, :], lam_init, None, ALU.add, None, [lam], [lam])
        ts("dve", subg[:, :], subg[:, :], 1.0 - lam_init, None, ALU.mult, None, [subg], [subg])
        bO = banks[6]
        bL = banks[7]
        NKP = NKT // 2
        pending = []
        o1s = [tile(f"T_o1{i}", [128, TB], F32) for i in range(2)]

        def load_head(h):
            load("sp", KT[h % 2], KT[h % 2][:, :], kT_d[h].rearrange("t d n -> (t d) n"), [r_kT])
            vr = vS_d.rearrange("(kt p) (h d) -> h p kt d", p=128, d=128)[h]
            half = NKT // 2
            load("sp", Vh[h % 2], Vh[h % 2][:, 0:half, :], vr[:, 0:half, :], [r_vS])
            load("sp", Vh[h % 2], Vh[h % 2][:, half:NKT, :], vr[:, half:NKT, :], [r_vS])

        load_head(0)
        ei = 0
        it = 0
        for h in range(8):
            if h + 1 < 8:
                load_head(h + 1)
            K_, V_ = KT[h % 2], Vh[h % 2]
            for qb in range(NB):
                q_ = qt[it % 2]
                load("sp", q_, q_[0:64, 0, :], qT_d[h][0][:, qb * TB:(qb + 1) * TB], [r_qT])
                load("sp", q_, q_[64:128, 1, :], qT_d[h][1][:, qb * TB:(qb + 1) * TB], [r_qT])
                steps = [(t, kp) for t in range(2) for kp in range(NKP)]

                def smm(i):
                    t, kp = steps[i]
                    for u_ in range(2):
                        b = banks[(i % 3) * 2 + u_]
                        kt = kp * 2 + u_
                        mm(b, b[:, :], K_[:, kt * 128:(kt + 1) * 128], q_[:, t, :], True, True, [K_, q_])

                o1 = o1s[it % 2]
                smm(0)
                if len(steps) > 1:
                    smm(1)
                for i, (t, kp) in enumerate(steps):
                    if i + 2 < len(steps):
                        smm(i + 2)
                    b0, b1 = banks[(i % 3) * 2], banks[(i % 3) * 2 + 1]
                    E = ET[ei % 3]
                    ei += 1
                    act(E[:, :], dbl[i % 3][:, 0:1024], AF.Exp, [b0, b1], [E], scale=0.125)
                    for u_ in range(2):
                        kt = kp * 2 + u_
                        mm(bO, bO[:, :], V_[:, kt, :], E[:, u_ * 512:(u_ + 1) * 512], kt == 0, kt == NKT - 1, [V_, E])
                        mm(bL, bL[:, :], ones_bf[:, :], E[:, u_ * 512:(u_ + 1) * 512], kt == 0, kt == NKT - 1, [ones_bf, E])
                    if kp == NKP - 1:
                        dO, dL = (o1, R1) if t == 0 else (o2, R2)
                        for src_b, dst_t in ((bL, dL), (bO, dO)):
                            S.op("dve", lambda e, src_b=src_b, dst_t=dst_t: e.tensor_copy(out=dst_t[:, :], in_=src_b[:, :]), [src_b.res], [dst_t.res])
                    if i == min(6, NKP - 2):
                        while pending:
                            pending.pop(0)(b0)
                recip(R1[:, :], R1[:, :], [R1], [R1])
                recip(R2[:, :], R2[:, :], [R2], [R2])
                ts("dve", R2[:, :], R2[:, :], lam[:, 0:1], None, ALU.mult, None, [R2, lam], [R2])
                tt("dve", o1[:, :], o1[:, :], R1[:, :], ALU.mult, [o1, R1], [o1])
                tt("dve", o2[:, :], o2[:, :], R2[:, :], ALU.mult, [o2, R2], [o2])
                tt("dve", o1[:, :], o1[:, :], o2[:, :], ALU.subtract, [o1, o2], [o1])

                def tail(bss, o1=o1, a_=aT[it % 2], qb=qb, h=h):
                    act(osq[:, :], o1[:, :], AF.Square, [o1], [osq])
                    mm(bss, bss[:, :], ones_bf[:, :], osq[:, :], True, True, [ones_bf, osq])
                    act(rt[:, :], bss[:, :], AF.Sqrt, [bss], [rt], scale=1.0 / 128, bias=EPS)
                    recip(rt[:, :], rt[:, :], [rt], [rt])
                    stt(a_[:, :], o1[:, :], subg[:, 0:1], rt[:, :], ALU.mult, ALU.mult, [o1, subg, rt], [a_])
                    store(a_, mixT_d[qb][:, h, :], a_[:, :], [r_mixT[qb]])
                pending.append(tail)
                it += 1
                conv_pump(4)
                while mod_jobs and (njobs - len(mod_jobs)) * 8 * NB < it * njobs:
                    mod_job(mwT, banks[1])
        while pending:
            pending.pop(0)(banks[0])
        while mod_jobs:
            mod_job(mwT, banks[1])
        mod_derive(0, False)
        mod_derive(1, True)
        mod_derive(1, False)

    def phaseB(l, wout_d, last):
        conv_pump(len(conv_q))
        wo_r = wout_d[0].rearrange("(c p) n -> p c n", p=128)
        wg_r = wg_d[l].rearrange("(c p) n -> p c n", p=128)
        wu_r = wu_d[l].rearrange("(c p) n -> p c n", p=128)
        wdn_r = wd_d[l].rearrange("(f p) n -> p f n", p=128)
        mh = tile(f"B{l}_mh", [128, DC, TB], BF16)
        xTb = tile(f"B{l}_xTb", [128, DC, TB], F32)
        yb = tile(f"B{l}_yb", [128, DC, TB], F32)
        actT = tile(f"B{l}_actT", [128, FC, TB], BF16)
        sqc = [tile(f"B{l}_sq{i}", [128, TB], BF16) for i in range(2)]
        rt = tile(f"B{l}_rt", [128, TB], F32)
        rstd = tile(f"B{l}_rstd", [128, TB], F32)
        tmpf = [tile(f"B{l}_tf{i}", [128, TB], F32) for i in range(2)]
        items_o, items_g, items_u, items_d = [], [], [], []
        for blk in range(NB):
            items_o += [wo_b[l][dc] for dc in range(DC)]
            items_g += [wg_b[l][f] for f in range(FC)]
            items_u += [wu_b[l][f] for f in range(FC)]
            for dc in range(DC):
                items_d += [wd_b[l][dc][:, 0:FC // 2, :], wd_b[l][dc][:, FC // 2:FC, :]]
        so = Stream(f"B{l}_wo", [128, DC, 128], items_o, nslot=3, nsplit=1, rres=[r_wB[l]])
        sg = Stream(f"B{l}_wg", [128, DC, 128], items_g, nslot=3, nsplit=1, rres=[r_wB[l]])
        su = Stream(f"B{l}_wu", [128, DC, 128], items_u, nslot=3, nsplit=1, rres=[r_wB[l]])
        sd = Stream(f"B{l}_wd", [128, FC // 2, 128], items_d, nslot=4, nsplit=1, rres=[r_wB[l]])
        bss = banks[7]

        def post_norm_residual(gg):
            act(rt[:, :], bss[:, :], AF.Sqrt, [bss], [rt], scale=1.0 / D, bias=EPS)
            recip(rstd[:, :], rt[:, :], [rt], [rstd])
            for dc in range(DC):
                stt(yb[:, dc, :], yb[:, dc, :], gg[:, dc:dc + 1], rstd[:, :], ALU.mult, ALU.mult, [yb, gg, rstd], [yb])
                tt("dve", xTb[:, dc, :], xTb[:, dc, :], yb[:, dc, :], ALU.add, [xTb, yb], [xTb])

        for blk in range(NB):
            load("sp", mh, mh[:, :, :], mixT_d[blk], [r_mixT[blk]])
            load("sp", xTb, xTb[:, :, :], xT_d[blk], [r_xT[blk]])
            for dc in range(DC):
                W = so.get(blk * DC + dc)
                b = nbank(0, 4)
                for c in range(DC):
                    mm(b, b[:, :], W[:, c, :], mh[:, c, :], c == 0, c == DC - 1, [W, mh])
                act(yb[:, dc, :], b[:, :], AF.Copy, [b], [yb])
                s = sqc[dc % 2]
                tt("dve", s[:, :], yb[:, dc, :], yb[:, dc, :], ALU.mult, [yb], [s])
                mm(bss, bss[:, :], ones_bf[:, :], s[:, :], dc == 0, dc == DC - 1, [ones_bf, s])
            post_norm_residual(ggm[l])
            norm_mod(xTb, TB, lambda c: sc1f[l][:, c:c + 1], lambda c: modc[l][:, 48 + c, 0:1],
                     mh, sqc, rt, rstd, tmpf, banks[6], [sc1f[l], modc[l]])
            for f in range(FC):
                Wg = sg.get(blk * FC + f)
                Wu = su.get(blk * FC + f)
                bg = nbank(0, 4)
                bu = nbank(0, 4)
                for c in range(DC):
                    mm(bg, bg[:, :], Wg[:, c, :], mh[:, c, :], c == 0, c == DC - 1, [Wg, mh])
                for c in range(DC):
                    mm(bu, bu[:, :], Wu[:, c, :], mh[:, c, :], c == 0, c == DC - 1, [Wu, mh])
                tf = tmpf[f % 2]
                act(tf[:, :], bg[:, :], AF.Silu, [bg], [tf])
                tt("dve", actT[:, f, :], tf[:, :], bu[:, :], ALU.mult, [tf, bu], [actT])
            for dc in range(DC):
                b = nbank(4, 6)
                for hf in range(2):
                    W = sd.get((blk * DC + dc) * 2 + hf)
                    for f2 in range(FC // 2):
                        f = hf * (FC // 2) + f2
                        mm(b, b[:, :], W[:, f2, :], actT[:, f, :], f == 0, f == FC - 1, [W, actT])
                act(yb[:, dc, :], b[:, :], AF.Copy, [b], [yb])
                s = sqc[dc % 2]
                tt("dve", s[:, :], yb[:, dc, :], yb[:, dc, :], ALU.mult, [yb], [s])
                mm(bss, bss[:, :], ones_bf[:, :], s[:, :], dc == 0, dc == DC - 1, [ones_bf, s])
            post_norm_residual(ggf[l])
            if not last:
                store(xTb, xT_d[blk], xTb[:, :, :], [r_xT[blk]])
            else:
                for t_ in range(4):
                    for c4 in range(4):
                        b = nbank(0, 4)
                        for i in range(4):
                            c = c4 * 4 + i
                            S.op("pe", lambda e, b=b, i=i, c=c, t_=t_: e.transpose(b[:, i * 128:(i + 1) * 128], xTb[:, c, t_ * 128:(t_ + 1) * 128], ident[:, :]),
                                 [xTb.res, ident.res], [b.res])
                        if c4 % 2 == 0:
                            act(yb[:, t_ * 4 + c4, :], b[:, :], AF.Copy, [b], [yb])
                        else:
                            S.op("dve", lambda e, b=b, c4=c4, t_=t_: e.tensor_copy(out=yb[:, t_ * 4 + c4, :], in_=b[:, :]), [b.res], [yb.res])
                    r0 = blk * TB + t_ * 128
                    store(yb, out_d[r0:r0 + 128, :].rearrange("p (a n) -> p a n", a=4), yb[:, t_ * 4:(t_ + 1) * 4, :], [r_out])

    def phaseC1():
        w_r = cdin_d[0].rearrange("(c p) n -> p c n", p=128)
        xTb = tile("C_xTb", [128, DC, TB], F32)
        hT = tile("C_hT", [128, DC, TB], BF16)
        sqc = [tile(f"C_sq{i}", [128, TB], BF16) for i in range(2)]
        rt = tile("C_rt", [128, TB], F32)
        rstd = tile("C_rstd", [128, TB], F32)
        tmpf = [tile(f"C_tf{i}", [128, TB], F32) for i in range(2)]
        sgm = [tile(f"C_sg{i}", [128, TB], F32) for i in range(2)]
        glu = [tile(f"C_glu{i}", [128, TB], F32) for i in range(2)]
        FT = [tile(f"C_FT{i}", [128, TB], BF16) for i in range(2)]
        GGb = [tile(f"C_GGb{i}", [128, 4, 256], BF16) for i in range(2)]
        ccs = tile("C_ccs", [128, 256], BF16)
        zt = tile("C_zt", [128, 16], F32)
        load("pool", ccs, ccs[:, :], ccs_d[:, :])
        S.op("dve", lambda e: e.memset(zt[:, :], 0.0), [], [zt.res])
        for j in range(8):
            store(zt, G_d[j][:, 0:16], zt[:, :], [r_G])
            store(zt, G_d[j][:, 16 + Tn:32 + Tn], zt[:, :], [r_G])
        items = []
        for blk in range(NB):
            items += [wC_b[k_] for k_ in range(24)]
        ws = Stream("C_w", [128, DC, 128], items, nslot=3, nsplit=1, rres=[r_wC])
        si = 0
        k = 0
        for blk in range(NB):
            load("sp", xTb, xTb[:, :, :], xT_d[blk], [r_xT[blk]])
            norm_mod(xTb, TB, lambda c: sc1m[1][:, c, 0:1], lambda c: modc[1][:, c, 0:1],
                     hT, sqc, rt, rstd, tmpf, banks[7], [sc1m[1], modc[1]])
            for j in range(8):
                Wa = ws.get(si)
                ba = nbank(0, 4)
                for c in range(DC):
                    mm(ba, ba[:, :], Wa[:, c, :], hT[:, c, :], c == 0, c == DC - 1, [Wa, hT])
                Wg = ws.get(si + 1)
                si += 2
                bg = nbank(0, 4)
                for c in range(DC):
                    mm(bg, bg[:, :], Wg[:, c, :], hT[:, c, :], c == 0, c == DC - 1, [Wg, hT])
                s_, g_ = sgm[k % 2], glu[k % 2]
                k += 1
                act(s_[:, :], bg[:, :], AF.Sigmoid, [bg], [s_])
                tt("dve", g_[:, :], s_[:, :], ba[:, :], ALU.mult, [s_, ba], [g_])
                store(g_, G_d[j][:, 16 + blk * TB:16 + (blk + 1) * TB], g_[:, :], [r_G])
            for g in range(8):
                Wf = ws.get(si)
                si += 1
                bf = nbank(0, 4)
                for c in range(DC):
                    mm(bf, bf[:, :], Wf[:, c, :], hT[:, c, :], c == 0, c == DC - 1, [Wf, hT])
                F_ = FT[g % 2]
                act(F_[:, :], bf[:, :], AF.Copy, [bf], [F_])
                GG = GGb[g % 2]
                for pr in range(2):
                    b2 = nbank(4, 6)
                    for tq in range(2):
                        t_ = pr * 2 + tq
                        mm(b2, b2[:, tq * 256:(tq + 1) * 256], F_[:, t_ * 128:(t_ + 1) * 128], ccs[:, :], True, True, [F_, ccs])
                    S.op("dve", lambda e, GG=GG, b2=b2, pr=pr: e.tensor_copy(out=GG[:, pr * 2:pr * 2 + 2, :], in_=b2[:, :].rearrange("p (a n) -> p a n", a=2)),
                         [b2.res], [GG.res])
                store(GG, GG_d[g][:, blk * 4:(blk + 1) * 4, :], GG[:, :, :], [r_GG])

    def phaseC2():
        yield
        tab = tile("F_tab", [128, NT, 2, TB], BF16)
        GGt = [tile(f"F_GG{i}", [128, NT, 256], BF16) for i in range(2)]
        yo = [tile(f"F_yo{i}", [128, TB], BF16) for i in range(2)]
        scale = 1.0 / float(np.sqrt(Tn * 128.0))
        it = 0
        for kb in range(NB):
            step = max(1, NT // 8)
            for a in range(0, NT, step):
                load("sp", tab, tab[:, a:a + step, :, :], dft_d[kb][:, a:a + step, :, :])
            for g in range(8):
                G_ = GGt[it % 2]
                load("sp", G_, G_[:, :, :], GG_d[g], [r_GG])
                b = nbank(0, 4)
                for nt in range(NT):
                    mm(b, b[:, :], G_[:, nt, 0:128], tab[:, nt, 0, :], nt == 0, False, [G_, tab])
                    mm(b, b[:, :], G_[:, nt, 128:256], tab[:, nt, 1, :], False, nt == NT - 1, [G_, tab])
                y_ = yo[it % 2]
                act(y_[:, :], b[:, :], AF.Copy, [b], [y_], scale=scale)
                store(y_, mixT_d[kb][:, 8 + g, :], y_[:, :], [r_mixT[kb]])
                it += 1
                yield

    def phaseC3():
        gin = [tile(f"V_gin{i}", [128, TB + 32], F32) for i in range(2)]
        yc = tile("V_yc", [128, 8, TB], F32)
        ybf = [tile(f"V_ybf{i}", [128, TB], BF16) for i in range(2)]
        ysq = [tile(f"V_ysq{i}", [128, TB], BF16) for i in range(2)]
        mu = tile("V_mu", [128, TB], F32)
        var = tile("V_var", [128, TB], F32)
        tmp = [tile(f"V_tmp{i}", [128, TB], F32) for i in range(2)]
        yo = tile("V_yo", [128, 8, TB], BF16)
        gbf = [tile(f"V_gbf{i}", [128, TB + 32], BF16) for i in range(2)]
        Dg = tile("V_Dg", [128, 8, 31, 128], BF16)
        dww = tile("V_dww", [128, 8, 31], F32)
        dwb = tile("V_dwb", [128, 8], F32)
        cng = tile("V_cng", [128, 8], F32)
        cnb = tile("V_cnb", [128, 8], F32)
        load("sp", dww, dww[:, :, :], dww_d[:, :, :])
        load("sp", dwb, dwb[:, :], dwb_d[:, :])
        load("sp", cng, cng[:, :], cng_d[:, :])
        load("sp", cnb, cnb[:, :], cnb_d[:, :])
        for j in range(8):
            for k in range(31):
                ts("dve", Dg[:, j, k, :], ident[:, :], dww[:, j, k:k + 1], None, ALU.mult, None, [ident, dww], [Dg])
        b1, b2 = banks[6], banks[7]
        it = 0
        for blk in range(NB):
            for j in range(8):
                it += 1
                g_ = gin[it % 2]
                load("sp", g_, g_[:, 0:TB + 30], G_d[j][:, 1 + blk * TB:1 + blk * TB + TB + 30], [r_G])
                gb_ = gbf[it % 2]
                act(gb_[:, 0:TB + 30], g_[:, 0:TB + 30], AF.Copy, [g_], [gb_])
                cb = nbank(4, 6)
                for k in range(31):
                    mm(cb, cb[:, :], Dg[:, j, k, :], gb_[:, k:k + TB], k == 0, k == 30, [Dg, gb_])
                yield
                act(yc[:, j, :], cb[:, :], AF.Identity, [cb, dwb], [yc], bias=dwb[:, j:j + 1])
                yb_, ys_ = ybf[j % 2], ysq[j % 2]
                act(yb_[:, :], yc[:, j, :], AF.Copy, [yc], [yb_])
                act(ys_[:, :], yc[:, j, :], AF.Square, [yc], [ys_])
                mm(b1, b1[:, :], ones_bf[:, :], yb_[:, :], j == 0, j == 7, [ones_bf, yb_])
                mm(b2, b2[:, :], ones_bf[:, :], ys_[:, :], j == 0, j == 7, [ones_bf, ys_])
            ts("dve", mu[:, :], b1[:, :], 1.0 / 1024, None, ALU.mult, None, [b1], [mu])
            tt("dve", var[:, :], mu[:, :], mu[:, :], ALU.mult, [mu], [var])
            stt(var[:, :], b2[:, :], 1.0 / 1024, var[:, :], ALU.mult, ALU.subtract, [b2, var], [var])
            act(var[:, :], var[:, :], AF.Sqrt, [var], [var], bias=EPS)
            recip(var[:, :], var[:, :], [var], [var])
            for j in range(8):
                t_ = tmp[j % 2]
                tt("dve", t_[:, :], yc[:, j, :], mu[:, :], ALU.subtract, [yc, mu], [t_])
                tt("dve", t_[:, :], t_[:, :], var[:, :], ALU.mult, [t_, var], [t_])
                act(yo[:, j, :], t_[:, :], AF.Silu, [t_, cng, cnb], [yo], scale=cng[:, j:j + 1], bias=cnb[:, j:j + 1])
            store(yo, mixT_d[blk][:, 0:8, :], yo[:, :, :], [r_mixT[blk]])

    def phaseC23():
        gens = [phaseC3(), phaseC2()]
        next(gens[1])
        while gens:
            for g_ in list(gens):
                try:
                    next(g_)
                except StopIteration:
                    gens.remove(g_)

    phases = [(phaseA,), (phaseATT,), (phaseB, 0, about_d, False), (phaseC1,), (phaseC23,), (phaseB, 1, cdout_d, True)]
    if only is not None:
        phases = [phases[i] for i in only]
    marks = [("phase0", S.q["pe"].n)]
    for ph in phases:
        with ExitStack() as st:
            stacks.append(st)
            ph[0](*ph[1:])
            S.barrier()
            stacks.pop()
        marks.append((ph[0].__name__ + str(ph[1] if len(ph) > 1 else ""), S.q["pe"].n))
    print("BUILD: sems", S.nsem, {k: q.n for k, q in S.q.items()}, "MARKS", marks)
    return nc


def _col(v, nch):
    return np.ascontiguousarray(np.asarray(v, np.float32).reshape(nch, 128).T)


def _consts(Tn):
    NB = Tn // TB
    NT = Tn // 128
    ident = np.eye(128, dtype=np.float32)
    P = np.zeros((128, 128), np.float32)
    for base in range(0, 128, 32):
        for i in range(16):
            P[base + i, base + i + 16] = 1.0
            P[base + i + 16, base + i] = 1.0
    permT = np.ascontiguousarray(P.T)
    n = np.arange(Tn)
    rows = (n // GRID_W).astype(np.float64)
    cols = (n % GRID_W).astype(np.float64)
    inv = 10000.0 ** (-np.arange(16, dtype=np.float64) / 16)
    ropeC = np.zeros((128, Tn), np.float64)
    ropeS = np.zeros((128, Tn), np.float64)
    for p in range(128):
        j = p % 64
        axis = j // 32
        f = j % 16
        second = (j % 32) >= 16
        ang = (rows if axis == 0 else cols) * inv[f]
        ropeC[p] = np.cos(ang)
        ropeS[p] = np.sin(ang) if second else -np.sin(ang)
    cc = np.arange(128, dtype=np.float64)
    B = 2 * np.pi * np.outer(cc, cc) / 128.0
    ccs = np.concatenate([np.cos(B), -np.sin(B)], axis=1).astype(np.float32)
    nn = np.arange(Tn, dtype=np.int64)
    prod = np.outer(nn, nn) % Tn
    A = 2 * np.pi * prod.astype(np.float64) / Tn
    Cn = np.cos(A).astype(np.float32)
    Sn = np.sin(A).astype(np.float32)
    dft = np.empty((NB, 128, NT, 2, TB), dtype=ml_dtypes.bfloat16)
    Cr = Cn.reshape(NT, 128, NB, TB).transpose(2, 1, 0, 3)
    Sr = Sn.reshape(NT, 128, NB, TB).transpose(2, 1, 0, 3)
    dft[:, :, :, 0, :] = Cr.astype(ml_dtypes.bfloat16)
    dft[:, :, :, 1, :] = Sr.astype(ml_dtypes.bfloat16)
    return dict(ident=ident, permT=permT, ropeC=ropeC.astype(np.float32), ropeS=ropeS.astype(np.float32),
                ccs=ccs, dft=dft)


def make_in_maps(inp, Tn, nb):
    f = lambda a: np.ascontiguousarray(np.asarray(a, np.float32))
    consts = _consts(Tn)
    bc = lambda v: np.ascontiguousarray(np.broadcast_to(np.asarray(v, np.float32), (128,) + np.asarray(v).shape))
    shared = dict(
        mod_w=f(inp["mod_w"]), ffn_w_gate=f(inp["ffn_w_gate"]), ffn_w_up=f(inp["ffn_w_up"]),
        ffn_w_down=f(inp["ffn_w_down"]), ab_w_in=f(inp["ab_w_in"]), ab_w_out=f(inp["ab_w_out"]),
        cd_w_in=f(inp["cd_w_in"]), cd_w_out=f(inp["cd_w_out"]),
        modb=np.ascontiguousarray(np.stack([_col(inp["mod_b"][l], 96) for l in range(2)], axis=1)),
        pmg=np.ascontiguousarray(np.stack([_col(inp["post_mix_g"][l], 16) for l in range(2)], axis=1)),
        pfg=np.ascontiguousarray(np.stack([_col(inp["post_ffn_g"][l], 16) for l in range(2)], axis=1)),
        lamv=bc(np.stack([inp["ab_lam_q1"][0], inp["ab_lam_k1"][0], inp["ab_lam_q2"][0], inp["ab_lam_k2"][0]])),
        subg=f(np.asarray(inp["ab_subln_g"][0]).reshape(128, 1)),
        vng=bc(inp["ab_vnorm_g"][0]), vnb=bc(inp["ab_vnorm_b"][0]),
        bsp=bc(inp["ab_b_spatial"][0]),
        wsT=f(np.transpose(np.asarray(inp["ab_w_spatial"][0]), (2, 0, 1))),
        dww=f(np.transpose(np.asarray(inp["cd_dw_w"][0]).reshape(31, 8, 128), (2, 1, 0))),
        dwb=_col(inp["cd_dw_b"][0], 8), cng=_col(inp["cd_norm_g"][0], 8), cnb=_col(inp["cd_norm_b"][0], 8),
        **consts,
    )
    maps = []
    for b in range(nb):
        m = dict(shared)
        m["x"] = f(inp["x"][b][:Tn])
        m["ctx"] = f(inp["ctx"][b])
        m["cvec"] = np.ascontiguousarray(np.stack([_col(inp["c"][b], 16), _col(inp["c_ctx"], 16)], axis=2))
        maps.append(m)
    return maps


def kernel(**inputs):
    Tn = 4096
    nb = 4
    nc = build(Tn)
    maps = make_in_maps(inputs, Tn, nb)
    zmap = {k: np.zeros_like(v) for k, v in maps[0].items()}
    in_maps = [maps[i] if i < nb else zmap for i in range(8)]
    res = run_bass_kernel_spmd(nc, in_maps, core_ids=list(range(8)))
    return np.stack([np.asarray(res.results[b]["out"], np.float32) for b in range(nb)], axis=0)
```

```python
import numpy as np
from contextlib import ExitStack
import ml_dtypes
import concourse.bass as bass
import concourse.mybir as mybir
from concourse.bass_utils import run_bass_kernel_spmd

F32, BF16 = mybir.dt.float32, mybir.dt.bfloat16
AF = mybir.ActivationFunctionType
ALU = mybir.AluOpType
AX = mybir.AxisListType

D = 2048
DC = 16
DFF = 5632
FC = 44
CTX = 256
EPS = 1e-6
GRID_W = 64
EPOCH = 30000
TB = 512
import os
CUT = int(os.environ.get('KCUT', '99'))
KV = os.environ.get('KV', '')


class Res:
    __slots__ = ("name", "w", "r", "dsem", "dcnt")

    def __init__(self, name):
        self.name = name
        self.w = None
        self.r = {}
        self.dsem = None
        self.dcnt = 0


class Queue:
    def __init__(self, name, h):
        self.name = name
        self.h = h
        self.n = 0
        self.sems = []
        self.known = {}


class Sched:
    def __init__(self, nc):
        self.nc = nc
        self.q = {
            "pe": Queue("pe", nc.tensor),
            "act": Queue("act", nc.scalar),
            "dve": Queue("dve", nc.vector),
            "pool": Queue("pool", nc.gpsimd),
            "sp": Queue("sp", nc.sync),
        }
        self.dres = []
        self.nsem = 0

    def _sem(self, name):
        self.nsem += 1
        return self.nc.alloc_semaphore(name)

    def _qsem(self, q, idx):
        e = idx // EPOCH
        while len(q.sems) <= e:
            q.sems.append(self._sem(f"q{q.name}{len(q.sems)}"))
        return q.sems[e], (idx % EPOCH) + 1, e

    def _wait(self, q, ev):
        if ev[0] == "q":
            _, qn, idx = ev
            if qn == q.name and qn in ("pe", "sp"):
                return
            src = self.q[qn]
            sem, val, e = self._qsem(src, idx)
            for (kq, ke), kv in q.known.items():
                if kq == qn and (ke > e or (ke == e and kv >= val)):
                    return
            q.h.wait_ge(sem, val)
            q.known[(qn, e)] = val
        else:
            res = ev[1]
            key = ("d", id(res))
            if q.known.get((key, 0), 0) >= res.dcnt:
                return
            q.h.wait_ge(res.dsem, res.dcnt)
            q.known[(key, 0)] = res.dcnt

    def _deps(self, q, reads, writes):
        for r in reads:
            if r.w is not None:
                self._wait(q, r.w)
        for w in writes:
            if w.w is not None:
                self._wait(q, w.w)
            for ev in list(w.r.values()):
                self._wait(q, ev)

    def op(self, qn, fn, reads=(), writes=()):
        q = self.q[qn]
        self._deps(q, reads, writes)
        sem, val, _ = self._qsem(q, q.n)
        ins = fn(q.h)
        ins.then_inc(sem, 1)
        me = ("q", qn, q.n)
        q.n += 1
        for r in reads:
            r.r[qn] = me
        for w in writes:
            w.w = me
            w.r = {}

    def dma(self, qn, out, in_, reads, writes, tag):
        q = self.q[qn]
        self._deps(q, reads, writes)
        if tag.dsem is None:
            tag.dsem = self._sem("d" + tag.name)
            self.dres.append(tag)
        ins = q.h.dma_start(out=out, in_=in_)
        ins.then_inc(tag.dsem, 16)
        tag.dcnt += 16
        me = ("d", tag)
        for r in reads:
            r.r[("d", id(tag))] = me
        for w in writes:
            w.w = me
            w.r = {}

    def barrier(self):
        for q in self.q.values():
            for o in self.q.values():
                if o is not q and o.n > 0:
                    self._wait(q, ("q", o.name, o.n - 1))
            for r in self.dres:
                self._wait(q, ("d", r))


class T:
    def __init__(self, h, name):
        self.t = h
        self.res = Res(name)
        self.name = name

    def __getitem__(self, k):
        return self.t[k]


def build(Tn, dbg=False, only=None):
    NB = Tn // TB
    NT = Tn // 128
    NK = Tn + CTX
    NKT = NK // 128
    nc = bass.Bass("TRN2", target_bir_lowering=False)
    S = Sched(nc)

    def din(name, shape, dt=F32):
        return nc.dram_tensor(name, list(shape), dt, kind="ExternalInput").ap()

    def dscr(name, shape, dt):
        kind = "ExternalOutput" if dbg else "Internal"
        return nc.dram_tensor(name, list(shape), dt, kind=kind).ap()

    x_d = din("x", [Tn, D])
    ctx_d = din("ctx", [CTX, D])
    cvec_d = din("cvec", [128, DC, 2])
    modw_d = din("mod_w", [2, D, 6 * D])
    modb_d = din("modb", [128, 2, 96])
    pmg_d = din("pmg", [128, 2, DC])
    pfg_d = din("pfg", [128, 2, DC])
    wg_d = din("ffn_w_gate", [2, D, DFF])
    wu_d = din("ffn_w_up", [2, D, DFF])
    wd_d = din("ffn_w_down", [2, DFF, D])
    abin_d = din("ab_w_in", [1, D, 5120])
    about_d = din("ab_w_out", [1, D, D])
    cdin_d = din("cd_w_in", [1, D, 3072])
    cdout_d = din("cd_w_out", [1, D, D])
    lamv_d = din("lamv", [128, 4, 64])
    subg_d = din("subg", [128, 1])
    vng_d = din("vng", [128, 1024])
    vnb_d = din("vnb", [128, 1024])
    bsp_d = din("bsp", [128, 8, 128])
    wsT_d = din("wsT", [128, 8, 128])
    dww_d = din("dww", [128, 8, 31])
    dwb_d = din("dwb", [128, 8])
    cng_d = din("cng", [128, 8])
    cnb_d = din("cnb", [128, 8])
    ident_d = din("ident", [128, 128])
    permT_d = din("permT", [128, 128])
    ropeC_d = din("ropeC", [128, Tn])
    ropeS_d = din("ropeS", [128, Tn])
    ccs_d = din("ccs", [128, 256])
    dft_d = din("dft", [NB, 128, NT, 2, TB], BF16)
    out_d = nc.dram_tensor("out", [Tn, D], F32, kind="ExternalOutput").ap()

    xT_d = dscr("xT_s", [NB, 128, DC, TB], F32)
    qT_d = dscr("qT_s", [8, 2, 64, Tn], BF16)
    kT_d = dscr("kT_s", [8, 2, 64, NK], BF16)
    vS_d = dscr("vS_s", [NK, 1024], BF16)
    mixT_d = dscr("mixT_s", [NB, 128, DC, TB], BF16)
    G_d = dscr("G_s", [8, 128, Tn + 32], F32)
    GG_d = dscr("GG_s", [8, 128, NT, 256], BF16)
    wA_b = nc.dram_tensor("wA_b", [10, 128, DC, 512], BF16, kind="Internal").ap()
    wC_b = nc.dram_tensor("wC_b", [24, 128, DC, 128], BF16, kind="Internal").ap()
    wo_b = [nc.dram_tensor(f"wo_b{l}", [DC, 128, DC, 128], BF16, kind="Internal").ap() for l in range(2)]
    wg_b = [nc.dram_tensor(f"wg_b{l}", [FC, 128, DC, 128], BF16, kind="Internal").ap() for l in range(2)]
    wu_b = [nc.dram_tensor(f"wu_b{l}", [FC, 128, DC, 128], BF16, kind="Internal").ap() for l in range(2)]
    wd_b = [nc.dram_tensor(f"wd_b{l}", [DC, 128, FC, 128], BF16, kind="Internal").ap() for l in range(2)]
    r_wA, r_wC = Res("wA"), Res("wC")
    r_wB = [Res("wB0"), Res("wB1")]
    r_xT = [Res(f"xT{b}") for b in range(NB)]
    r_mixT = [Res(f"mixT{b}") for b in range(NB)]
    r_qT, r_kT, r_vS, r_G, r_GG = Res("qT"), Res("kT"), Res("vS"), Res("G"), Res("GG")
    r_out = Res("out")

    stacks = [ExitStack()]

    def tile(name, shape, dt, psum=False):
        if psum:
            return T(nc.alloc_psum_tensor("ps_" + name, list(shape), dt), name)
        return T(stacks[-1].enter_context(nc.sbuf_tensor("sb_" + name, list(shape), dt)), name)

    class BankView:
        def __init__(self, base, off, name):
            self.base, self.off, self.name = base, off, name
            self.res = Res(name)

        def __getitem__(self, k):
            p, c = k
            a = 0 if c.start is None else c.start
            b = 512 if c.stop is None else c.stop
            return self.base[p, self.off + a:self.off + b]

    dbl = [nc.alloc_psum_tensor(f"ps_dbl{i}", [128, 1024], F32) for i in range(4)]
    banks = [BankView(dbl[i // 2], (i % 2) * 512, f"bank{i}") for i in range(8)]

    def mm(bank, out, lhsT, rhs, start, stop, reads):
        S.op("pe", lambda e: e.matmul(out, lhsT=lhsT, rhs=rhs, start=start, stop=stop), [x.res for x in reads], [bank.res])

    def act(out, in_, func, reads, writes, scale=1.0, bias=0.0):
        S.op("act", lambda e: e.activation(out=out, in_=in_, func=func, bias=bias, scale=scale),
             [x.res for x in reads], [x.res for x in writes])

    def tt(eng, out, in0, in1, op, reads, writes):
        S.op(eng, lambda e: e.tensor_tensor(out=out, in0=in0, in1=in1, op=op),
             [x.res for x in reads], [x.res for x in writes])

    def ts(eng, out, in0, s1, s2, op0, op1, reads, writes):
        if op1 is None:
            S.op(eng, lambda e: e.tensor_scalar(out=out, in0=in0, scalar1=s1, scalar2=None, op0=op0),
                 [x.res for x in reads], [x.res for x in writes])
        else:
            S.op(eng, lambda e: e.tensor_scalar(out=out, in0=in0, scalar1=s1, scalar2=s2, op0=op0, op1=op1),
                 [x.res for x in reads], [x.res for x in writes])

    def stt(out, in0, scalar, in1, op0, op1, reads, writes):
        S.op("dve", lambda e: e.scalar_tensor_tensor(out=out, in0=in0, scalar=scalar, in1=in1, op0=op0, op1=op1),
             [x.res for x in reads], [x.res for x in writes])

    def recip(out, in_, reads, writes):
        S.op("dve", lambda e: e.reciprocal(out=out, in_=in_), [x.res for x in reads], [x.res for x in writes])

    def load(qn, dst, out_ap, in_ap, rres=()):
        S.dma(qn, out_ap, in_ap, list(rres), [dst.res], dst.res)

    def store(src, out_ap, in_ap, wres):
        S.dma("sp", out_ap, in_ap, [src.res], list(wres), src.res)

    ones_bf = tile("ones_bf", [128, 128], BF16)
    ident = tile("ident", [128, 128], F32)
    permT = tile("permT", [128, 128], BF16)
    S.op("dve", lambda e: e.memset(ones_bf[:, :], 1.0), [], [ones_bf.res])
    load("sp", ident, ident[:, :], ident_d[:, :])
    load("pool", permT, permT[:, :], permT_d[:, :])
    cvec = tile("cvec", [128, DC, 2], F32)
    scv = tile("scv", [128, DC, 2], F32)
    load("sp", cvec, cvec[:, :, :], cvec_d[:, :, :])
    act(scv[:, :, :], cvec[:, :, :], AF.Silu, [cvec], [scv])
    modb = tile("modb", [128, 2, 96], F32)
    pmg = tile("pmg", [128, 2, DC], F32)
    pfg = tile("pfg", [128, 2, DC], F32)
    load("sp", modb, modb[:, :, :], modb_d[:, :, :])
    load("sp", pmg, pmg[:, :, :], pmg_d[:, :, :])
    load("sp", pfg, pfg[:, :, :], pfg_d[:, :, :])
    modc = [tile(f"modc{l}", [128, 96, 2], F32) for l in range(2)]
    sc1m = [tile(f"sc1m{l}", [128, DC, 2], F32) for l in range(2)]
    sc1f = [tile(f"sc1f{l}", [128, DC], F32) for l in range(2)]
    ggm = [tile(f"ggm{l}", [128, DC], F32) for l in range(2)]
    ggf = [tile(f"ggf{l}", [128, DC], F32) for l in range(2)]

    conv_q = []

    def conv(dst, src, res):
        conv_q.append((dst, src, res))

    def conv_pump(n):
        for _ in range(n):
            if not conv_q:
                return
            dst, src, res = conv_q.pop(0)
            S.dma("pool", dst, src, [], [res], res)

    def convert_A():
        w_r = abin_d[0].rearrange("(c p) n -> p c n", p=128)
        for s_ in range(10):
            for a in range(0, DC, 4):
                conv(wA_b[s_][:, a:a + 4, :], w_r[:, a:a + 4, s_ * 512:(s_ + 1) * 512], r_wA)

    def convert_B(l, wout_d):
        wo_r = wout_d[0].rearrange("(c p) n -> p c n", p=128)
        wg_r = wg_d[l].rearrange("(c p) n -> p c n", p=128)
        wu_r = wu_d[l].rearrange("(c p) n -> p c n", p=128)
        wdn_r = wd_d[l].rearrange("(f p) n -> p f n", p=128)
        for dc in range(DC):
            conv(wo_b[l][dc], wo_r[:, :, dc * 128:(dc + 1) * 128], r_wB[l])
        for f in range(FC):
            conv(wg_b[l][f], wg_r[:, :, f * 128:(f + 1) * 128], r_wB[l])
            conv(wu_b[l][f], wu_r[:, :, f * 128:(f + 1) * 128], r_wB[l])
        for dc in range(DC):
            for a in range(0, FC, 11):
                conv(wd_b[l][dc][:, a:a + 11, :], wdn_r[:, a:a + 11, dc * 128:(dc + 1) * 128], r_wB[l])

    def convert_C():
        w_r = cdin_d[0].rearrange("(c p) n -> p c n", p=128)
        k_ = 0
        for j in range(8):
            conv(wC_b[k_], w_r[:, :, j * 128:(j + 1) * 128], r_wC)
            conv(wC_b[k_ + 1], w_r[:, :, 1024 + j * 128:1024 + (j + 1) * 128], r_wC)
            k_ += 2
        for g in range(8):
            conv(wC_b[k_], w_r[:, :, 2048 + g * 128:2048 + (g + 1) * 128], r_wC)
            k_ += 1

    convert_A()
    conv_pump(len(conv_q))
    convert_B(0, about_d)
    convert_C()
    convert_B(1, cdout_d)

    mod_jobs = [(l, sl) for l in range(2) for sl in range(24)]
    mod_it = [0]

    def mod_job(mw, bank):
        if not mod_jobs:
            return
        l, sl = mod_jobs.pop(0)
        mwr = modw_d[l].rearrange("(c p) n -> p c n", p=128)
        w = mw[mod_it[0] % 2]
        mod_it[0] += 1
        for c4 in range(4):
            load("sp", w, w[:, c4 * 4:(c4 + 1) * 4, :], mwr[:, c4 * 4:(c4 + 1) * 4, sl * 512:(sl + 1) * 512])
        for jj in range(4):
            for c in range(DC):
                mm(bank, bank[:, 2 * jj:2 * jj + 2], w[:, c, jj * 128:(jj + 1) * 128], scv[:, c, :],
                   c == 0, c == DC - 1, [w, scv])
        tt("dve", modc[l][:, sl * 4:(sl + 1) * 4, :], bank[:, 0:8].rearrange("p (a b) -> p a b", b=2),
           modb[:, l, sl * 4:(sl + 1) * 4].unsqueeze(2).to_broadcast([128, 4, 2]), ALU.add, [bank, modb], [modc[l]])

    def mod_derive(l, first):
        if first:
            ts("dve", sc1m[l][:, :, :], modc[l][:, 16:32, :], 1.0, None, ALU.add, None, [modc[l]], [sc1m[l]])
            tt("dve", ggm[l][:, :], modc[l][:, 32:48, 0], pmg[:, l, :], ALU.mult, [modc[l], pmg], [ggm[l]])
        else:
            ts("dve", sc1f[l][:, :], modc[l][:, 64:80, 0], 1.0, None, ALU.add, None, [modc[l]], [sc1f[l]])
            tt("dve", ggf[l][:, :], modc[l][:, 80:96, 0], pfg[:, l, :], ALU.mult, [modc[l], pfg], [ggf[l]])

    with ExitStack() as st0:
        stacks.append(st0)
        mw0 = [tile(f"mw{i}", [128, DC, 512], F32) for i in range(2)]
        for _ in range(12 if "m" in KV else 48):
            mod_job(mw0, banks[_ % 2])
        mod_derive(0, True)
        S.barrier()
        stacks.pop()

    def norm_mod(xTb, TBk, sc_ap, sh_ap, hT, sqc, rt, rstd, tmpf, ssbank, rd):
        for c in range(DC):
            s = sqc[c % 2]
            act(s[:, :TBk], xTb[:, c, :TBk], AF.Square, [xTb], [s])
            mm(ssbank, ssbank[:, :TBk], ones_bf[:, :], s[:, :TBk], c == 0, c == DC - 1, [ones_bf, s])
        act(rt[:, :TBk], ssbank[:, :TBk], AF.Sqrt, [ssbank], [rt], scale=1.0 / D, bias=EPS)
        recip(rstd[:, :TBk], rt[:, :TBk], [rt], [rstd])
        for c in range(DC):
            tf = tmpf[c % 2]
            stt(tf[:, :TBk], xTb[:, c, :TBk], sc_ap(c), rstd[:, :TBk], ALU.mult, ALU.mult, [xTb, rstd] + rd, [tf])
            act(hT[:, c, :TBk], tf[:, :TBk], AF.Identity, [tf] + rd, [hT], bias=sh_ap(c))

    def gelu(zb, zap, out_ap, outres, g1, g2, n, zsb):
        act(zsb[:, :n], zap, AF.Copy, [zb], [zsb])
        act(g1[:, :n], zsb[:, :n], AF.Square, [zsb], [g1])
        ts("dve", g1[:, :n], g1[:, :n], 0.044715, 1.0, ALU.mult, ALU.add, [g1], [g1])
        tt("dve", g1[:, :n], g1[:, :n], zsb[:, :n], ALU.mult, [g1, zsb], [g1])
        act(g2[:, :n], g1[:, :n], AF.Sigmoid, [g1], [g2], scale=1.5957691216057308)
        tt("dve", out_ap, g2[:, :n], zsb[:, :n], ALU.mult, [g2, zsb], [outres])

    class Stream:
        def __init__(self, name, shape, items, nslot=2, nsplit=4, rres=()):
            self.slots = [tile(f"{name}{i}", shape, BF16) for i in range(nslot)]
            self.items = items
            self.rres = list(rres)
            self.issued = 0
            self.nsplit = nsplit

        def _issue(self):
            i = self.issued
            if i >= len(self.items):
                return
            if "n" in KV and i >= len(self.slots):
                self.issued += 1
                return
            sl = self.slots[i % len(self.slots)]
            src = self.items[i]
            n1 = sl.t.shape[1]
            step = max(1, n1 // self.nsplit)
            for a in range(0, n1, step):
                load("pool", sl, sl[:, a:a + step, :], src[:, a:a + step, :], self.rres)
            self.issued += 1

        def get(self, i):
            while self.issued <= i + len(self.slots) - 1:
                if self.issued >= len(self.items):
                    break
                self._issue()
            return self.slots[i % len(self.slots)]

    bk = [0]

    def nbank(lo=0, hi=8):
        b = banks[lo + bk[0] % (hi - lo)]
        bk[0] += 1
        return b

    def transpose_in(src_d, row0, TBk, xin, xTb):
        k = 0
        for t_ in range(TBk // 128):
            xi = xin[t_ % 2]
            for hh in range(2):
                load("sp", xi, xi[:, hh * 1024:(hh + 1) * 1024], src_d[row0 + t_ * 128:row0 + (t_ + 1) * 128, hh * 1024:(hh + 1) * 1024])
            for c4 in range(4):
                b = nbank(0, 4)
                for i in range(4):
                    c = c4 * 4 + i
                    S.op("pe", lambda e, b=b, i=i, c=c, xi=xi: e.transpose(b[:, i * 128:(i + 1) * 128], xi[:, c * 128:(c + 1) * 128], ident[:, :]),
                         [xi.res, ident.res], [b.res])
                src = b[:, :].rearrange("p (a n) -> p a n", a=4)
                dst = xTb[:, c4 * 4:(c4 + 1) * 4, t_ * 128:(t_ + 1) * 128]
                if k % 2 == 0:
                    S.op("act", lambda e, dst=dst, src=src: e.copy(out=dst, in_=src), [b.res], [xTb.res])
                else:
                    S.op("dve", lambda e, dst=dst, src=src: e.tensor_copy(out=dst, in_=src), [b.res], [xTb.res])
                k += 1

    def phaseA():
        w_r = abin_d[0].rearrange("(c p) n -> p c n", p=128)
        xin = [tile(f"A_xin{i}", [128, D], F32) for i in range(2)]
        xTb = tile("A_xTb", [128, DC, TB], F32)
        hT = tile("A_hT", [128, DC, TB], BF16)
        sqc = [tile(f"A_sq{i}", [128, TB], BF16) for i in range(2)]
        rt = tile("A_rt", [128, TB], F32)
        rstd = tile("A_rstd", [128, TB], F32)
        tmpf = [tile(f"A_tf{i}", [128, TB], F32) for i in range(2)]
        rC = tile("A_rC", [128, TB], F32)
        rS = tile("A_rS", [128, TB], F32)
        qsb = [tile(f"A_qsb{i}", [128, TB], BF16) for i in range(2)]
        qf = [tile(f"A_qf{i}", [128, TB], F32) for i in range(2)]
        zsb = tile("A_zsb", [128, 512], F32)
        t1 = [tile(f"A_t1{i}", [128, TB], F32) for i in range(2)]
        t2 = [tile(f"A_t2{i}", [128, TB], F32) for i in range(2)]
        rot = [tile(f"A_rot{i}", [128, TB], BF16) for i in range(2)]
        vsb = [tile(f"A_vsb{i}", [128, 512], BF16) for i in range(2)]
        g1 = tile("A_g1", [128, 512], F32)
        g2 = tile("A_g2", [128, 512], F32)
        vgf = tile("A_vgf", [128, 512], F32)
        vt = tile("A_vt", [128, 512], F32)
        vln = [tile(f"A_vln{i}", [128, 4, 128], BF16) for i in range(2)]
        svt = [tile(f"A_svt{i}", [128, 512], F32) for i in range(2)]
        vi = [0]
        st_ = tile("A_st", [128, 8], F32)
        uTb = tile("A_uTb", [128, 8, TB], BF16)
        sTb = tile("A_sTb", [128, 8, TB], BF16)
        vng = tile("A_vng", [128, 1024], F32)
        vnb = tile("A_vnb", [128, 1024], F32)
        bsp = tile("A_bsp", [128, 8, 128], F32)
        wsT = tile("A_wsT", [128, 8, 128], BF16)
        load("sp", vng, vng[:, :], vng_d[:, :])
        load("sp", vnb, vnb[:, :], vnb_d[:, :])
        load("sp", bsp, bsp[:, :, :], bsp_d[:, :, :])
        load("pool", wsT, wsT[:, :, :], wsT_d[:, :, :])

        sched = []
        for blk in range(NB):
            sched += [(blk, s) for s in range(10)]
        sched += [(NB, s) for s in (2, 3, 4, 5)]
        items = [wA_b[s] for (_, s) in sched]
        ws = Stream("A_w", [128, DC, 512], items, nsplit=1, rres=[r_wA])
        si = 0
        cnt = [0]
        deferred = []

        def run_deferred():
            while deferred:
                deferred.pop(0)()

        for blk in range(NB + 1):
            is_ctx = blk == NB
            TBk = CTX if is_ctx else TB
            mi = 1 if is_ctx else 0
            if is_ctx:
                transpose_in(ctx_d, 0, TBk, xin, xTb)
            else:
                transpose_in(x_d, blk * TB, TBk, xin, xTb)
                store(xTb, xT_d[blk], xTb[:, :, :], [r_xT[blk]])
                load("sp", rC, rC[:, :], ropeC_d[:, blk * TB:(blk + 1) * TB])
                load("sp", rS, rS[:, :], ropeS_d[:, blk * TB:(blk + 1) * TB])
            norm_mod(xTb, TBk, lambda c: sc1m[0][:, c, mi:mi + 1], lambda c: modc[0][:, c, mi:mi + 1],
                     hT, sqc, rt, rstd, tmpf, banks[7], [sc1m[0], modc[0]])
            slabs = (2, 3, 4, 5) if is_ctx else range(10)
            if CUT <= 2 or (is_ctx and "c" in KV):
                slabs = ()
            elif CUT < 90:
                slabs = [s for s in slabs if s < (CUT - 2) * 2]
            for s in slabs:
                W = ws.get(si)
                si += 1
                conv_pump(2)
                if s < 4:
                    is_k = s >= 2
                    for j in range(4):
                        hh = (s % 2) * 4 + j
                        b = nbank(0, 4)
                        for c in range(DC):
                            mm(b, b[:, :TBk], W[:, c, j * 128:(j + 1) * 128], hT[:, c, :TBk], c == 0, c == DC - 1, [W, hT])
                        i2 = cnt[0] % 2
                        cnt[0] += 1
                        if is_ctx:
                            act(rot[i2][:, :TBk], b[:, :TBk], AF.Copy, [b], [rot[i2]])
                            store(rot[i2], kT_d[hh].rearrange("t d n -> (t d) n")[:, Tn:Tn + CTX], rot[i2][:, :TBk], [r_kT])
                            continue
                        act(qf[i2][:, :], b[:, :], AF.Copy, [b], [qf[i2]])
                        act(qsb[i2][:, :], b[:, :], AF.Copy, [b], [qsb[i2]])
                        def rope_tail(i2=i2, hh=hh, is_k=is_k, blk=blk):
                            pb = nbank(4, 6)
                            mm(pb, pb[:, :], permT[:, :], qsb[i2][:, :], True, True, [permT, qsb[i2]])
                            tt("dve", t1[i2][:, :], qf[i2][:, :], rC[:, :], ALU.mult, [qf[i2], rC], [t1[i2]])
                            tt("dve", t2[i2][:, :], pb[:, :], rS[:, :], ALU.mult, [pb, rS], [t2[i2]])
                            tt("dve", rot[i2][:, :], t1[i2][:, :], t2[i2][:, :], ALU.add, [t1[i2], t2[i2]], [rot[i2]])
                            dst = (kT_d if is_k else qT_d)[hh].rearrange("t d n -> (t d) n")[:, blk * TB:(blk + 1) * TB]
                            store(rot[i2], dst, rot[i2][:, :], [r_kT if is_k else r_qT])
                        run_deferred()
                        deferred.append(rope_tail)
                elif s < 6:
                    row0 = Tn if is_ctx else blk * TB
                    for t_ in range(TBk // 128):
                        b = nbank(0, 4)
                        for c in range(DC):
                            mm(b, b[:, :], hT[:, c, t_ * 128:(t_ + 1) * 128], W[:, c, :], c == 0, c == DC - 1, [W, hT])
                        i2 = cnt[0] % 2
                        cnt[0] += 1
                        run_deferred()
                        act(vsb[i2][:, :], b[:, :], AF.Copy, [b], [vsb[i2]])
                        store(vsb[i2], vS_d[row0 + t_ * 128:row0 + (t_ + 1) * 128, (s - 4) * 512:(s - 3) * 512], vsb[i2][:, :], [r_vS])
                elif s < 8:
                    for j in range(4):
                        g = (s - 6) * 4 + j
                        b = nbank(0, 4)
                        for c in range(DC):
                            mm(b, b[:, :], W[:, c, j * 128:(j + 1) * 128], hT[:, c, :], c == 0, c == DC - 1, [W, hT])
                        run_deferred()
                        gelu(b, b[:, :], uTb[:, g, :], uTb, g1, g2, TB, zsb)
                else:
                    g0 = (s - 8) * 4
                    for t_ in range(4):
                        b = nbank(0, 4)
                        for c in range(DC):
                            mm(b, b[:, :], hT[:, c, t_ * 128:(t_ + 1) * 128], W[:, c, :], c == 0, c == DC - 1, [W, hT])
                        gelu(b, b[:, :], vgf[:, :], vgf, g1, g2, 512, zsb)
                        v3 = vgf[:, :].rearrange("p (g c) -> p g c", g=4)
                        S.op("dve", lambda e, v3=v3: e.tensor_reduce(out=st_[:, 0:4], in_=v3, axis=AX.X, op=ALU.add), [vgf.res], [st_.res])
                        act(vt[:, :], vgf[:, :], AF.Square, [vgf], [vt])
                        vt3 = vt[:, :].rearrange("p (g c) -> p g c", g=4)
                        S.op("dve", lambda e, vt3=vt3: e.tensor_reduce(out=st_[:, 4:8], in_=vt3, axis=AX.X, op=ALU.add), [vt.res], [st_.res])
                        ts("dve", st_[:, 0:4], st_[:, 0:4], 1.0 / 128, None, ALU.mult, None, [st_], [st_])
                        tt("dve", vt[:, 0:4], st_[:, 0:4], st_[:, 0:4], ALU.mult, [st_], [vt])
                        stt(st_[:, 4:8], st_[:, 4:8], 1.0 / 128, vt[:, 0:4], ALU.mult, ALU.subtract, [st_, vt], [st_])
                        act(st_[:, 4:8], st_[:, 4:8], AF.Sqrt, [st_], [st_], bias=EPS)
                        recip(st_[:, 4:8], st_[:, 4:8], [st_], [st_])
                        tt("dve", vt3, v3, st_[:, 0:4].unsqueeze(2).to_broadcast([128, 4, 128]), ALU.subtract, [vgf, st_], [vt])
                        tt("dve", vt3, vt3, st_[:, 4:8].unsqueeze(2).to_broadcast([128, 4, 128]), ALU.mult, [vt, st_], [vt])
                        tt("dve", vt[:, :], vt[:, :], vng[:, g0 * 128:(g0 + 4) * 128], ALU.mult, [vt, vng], [vt])
                        run_deferred()
                        vl = vln[vi[0] % 2]
                        vi[0] += 1
                        tt("dve", vl[:, :, :], vt3, vnb[:, g0 * 128:(g0 + 4) * 128].rearrange("p (g c) -> p g c", g=4), ALU.add, [vt, vnb], [vl])

                        def spatial_tail(vl=vl, g0=g0, t_=t_):
                            sb_ = nbank(4, 6)
                            for gi in range(4):
                                mm(sb_, sb_[:, gi * 128:(gi + 1) * 128], vl[:, gi, :], wsT[:, g0 + gi, :], True, True, [vl, wsT])
                            sv = svt[t_ % 2]
                            sv3 = sv[:, :].rearrange("p (g c) -> p g c", g=4)
                            tt("dve", sv3, sb_[:, :].rearrange("p (g c) -> p g c", g=4), bsp[:, g0:g0 + 4, :], ALU.add, [sb_, bsp], [sv])
                            tt("dve", sTb[:, g0:g0 + 4, t_ * 128:(t_ + 1) * 128], sv3, uTb[:, g0:g0 + 4, t_ * 128:(t_ + 1) * 128], ALU.mult, [sv, uTb], [sTb])
                        deferred.append(spatial_tail)
            run_deferred()
            if not is_ctx and CUT > 6:
                store(sTb, mixT_d[blk][:, 8:16, :], sTb[:, :, :], [r_mixT[blk]])

    def phaseATT():
        KT = [tile(f"T_KT{i}", [128, NK], BF16) for i in range(2)]
        Vh = [tile(f"T_V{i}", [128, NKT, 128], BF16) for i in range(2)]
        qt = [tile(f"T_q{i}", [128, 2, TB], BF16) for i in range(2)]
        for q__ in qt:
            S.op("dve", lambda e, q__=q__: e.memset(q__[:, :, :], 0.0), [], [q__.res])
        ET = [tile(f"T_E{i}", [128, 2 * TB], BF16) for i in range(3)]
        R1 = tile("T_R1", [128, TB], F32)
        R2 = tile("T_R2", [128, TB], F32)
        o2 = tile("T_o2", [128, TB], F32)
        osq = tile("T_osq", [128, TB], BF16)
        rt = tile("T_rt", [128, TB], F32)
        aT = [tile(f"T_aT{i}", [128, TB], BF16) for i in range(2)]
        mwT = [tile(f"T_mw{i}", [128, DC, 512], F32) for i in range(2)]
        njobs = len(mod_jobs)
        lamv = tile("T_lamv", [128, 4, 64], F32)
        lt = tile("T_lt", [128, 2, 64], F32)
        l2 = tile("T_l2", [128, 2], F32)
        lam = tile("T_lam", [128, 1], F32)
        subg = tile("T_subg", [128, 1], F32)
        load("sp", lamv, lamv[:, :, :], lamv_d[:, :, :])
        load("sp", subg, subg[:, :], subg_d[:, :])
        lam_init = 0.8 - 0.6 * float(np.exp(-0.3 * 0))
        tt("dve", lt[:, 0, :], lamv[:, 0, :], lamv[:, 1, :], ALU.mult, [lamv], [lt])
        tt("dve", lt[:, 1, :], lamv[:, 2, :], lamv[:, 3, :], ALU.mult, [lamv], [lt])
        S.op("dve", lambda e: e.tensor_reduce(out=l2[:, :], in_=lt[:, :, :], axis=AX.X, op=ALU.add), [lt.res], [l2.res])
        act(l2[:, :], l2[:, :], AF.Exp, [l2], [l2])
        tt("dve", lam[:, :], l2[:, 0:1], l2[:, 1:2], ALU.subtract, [l2], [lam])
        ts("dve", lam[:, :], lam[:, :], lam_init, None, ALU.add, None, [lam], [lam])
        ts("dve", subg[:, :], subg[:, :], 1.0 - lam_init, None, ALU.mult, None, [subg], [subg])
        bO = banks[6]
        bL = banks[7]
        NKP = NKT // 2
        pending = []
        o1s = [tile(f"T_o1{i}", [128, TB], F32) for i in range(2)]

        def load_head(h):
            load("sp", KT[h % 2], KT[h % 2][:, :], kT_d[h].rearrange("t d n -> (t d) n"), [r_kT])
            vr = vS_d.rearrange("(kt p) (h d) -> h p kt d", p=128, d=128)[h]
            half = NKT // 2
            load("sp", Vh[h % 2], Vh[h % 2][:, 0:half, :], vr[:, 0:half, :], [r_vS])
            load("sp", Vh[h % 2], Vh[h % 2][:, half:NKT, :], vr[:, half:NKT, :], [r_vS])

        load_head(0)
        ei = 0
        it = 0
        for h in range(8):
            if h + 1 < 8:
                load_head(h + 1)
            K_, V_ = KT[h % 2], Vh[h % 2]
            for qb in range(NB):
                q_ = qt[it % 2]
                load("sp", q_, q_[0:64, 0, :], qT_d[h][0][:, qb * TB:(qb + 1) * TB], [r_qT])
                load("sp", q_, q_[64:128, 1, :], qT_d[h][1][:, qb * TB:(qb + 1) * TB], [r_qT])
                steps = [(t, kp) for t in range(2) for kp in range(NKP)]

                def smm(i):
                    t, kp = steps[i]
                    for u_ in range(2):
                        b = banks[(i % 3) * 2 + u_]
                        kt = kp * 2 + u_
                        mm(b, b[:, :], K_[:, kt * 128:(kt + 1) * 128], q_[:, t, :], True, True, [K_, q_])

                o1 = o1s[it % 2]
                smm(0)
                if len(steps) > 1:
                    smm(1)
                for i, (t, kp) in enumerate(steps):
                    if i + 2 < len(steps):
                        smm(i + 2)
                    b0, b1 = banks[(i % 3) * 2], banks[(i % 3) * 2 + 1]
                    E = ET[ei % 3]
                    ei += 1
                    act(E[:, :], dbl[i % 3][:, 0:1024], AF.Exp, [b0, b1], [E], scale=0.125)
                    for u_ in range(2):
                        kt = kp * 2 + u_
                        mm(bO, bO[:, :], V_[:, kt, :], E[:, u_ * 512:(u_ + 1) * 512], kt == 0, kt == NKT - 1, [V_, E])
                        mm(bL, bL[:, :], ones_bf[:, :], E[:, u_ * 512:(u_ + 1) * 512], kt == 0, kt == NKT - 1, [ones_bf, E])
                    if kp == NKP - 1:
                        dO, dL = (o1, R1) if t == 0 else (o2, R2)
                        for src_b, dst_t in ((bL, dL), (bO, dO)):
                            S.op("dve", lambda e, src_b=src_b, dst_t=dst_t: e.tensor_copy(out=dst_t[:, :], in_=src_b[:, :]), [src_b.res], [dst_t.res])
                    if i == min(6, NKP - 2):
                        while pending:
                            pending.pop(0)(b0)
                recip(R1[:, :], R1[:, :], [R1], [R1])
                recip(R2[:, :], R2[:, :], [R2], [R2])
                ts("dve", R2[:, :], R2[:, :], lam[:, 0:1], None, ALU.mult, None, [R2, lam], [R2])
                tt("dve", o1[:, :], o1[:, :], R1[:, :], ALU.mult, [o1, R1], [o1])
                tt("dve", o2[:, :], o2[:, :], R2[:, :], ALU.mult, [o2, R2], [o2])
                tt("dve", o1[:, :], o1[:, :], o2[:, :], ALU.subtract, [o1, o2], [o1])

                def tail(bss, o1=o1, a_=aT[it % 2], qb=qb, h=h):
                    act(osq[:, :], o1[:, :], AF.Square, [o1], [osq])
                    mm(bss, bss[:, :], ones_bf[:, :], osq[:, :], True, True, [ones_bf, osq])
                    act(rt[:, :], bss[:, :], AF.Sqrt, [bss], [rt], scale=1.0 / 128, bias=EPS)
                    recip(rt[:, :], rt[:, :], [rt], [rt])
                    stt(a_[:, :], o1[:, :], subg[:, 0:1], rt[:, :], ALU.mult, ALU.mult, [o1, subg, rt], [a_])
                    store(a_, mixT_d[qb][:, h, :], a_[:, :], [r_mixT[qb]])
                pending.append(tail)
                it += 1
                conv_pump(4)
                while mod_jobs and (njobs - len(mod_jobs)) * 8 * NB < it * njobs:
                    mod_job(mwT, banks[1])
        while pending:
            pending.pop(0)(banks[0])
        while mod_jobs:
            mod_job(mwT, banks[1])
        mod_derive(0, False)
        mod_derive(1, True)
        mod_derive(1, False)

    def phaseB(l, wout_d, last):
        conv_pump(len(conv_q))
        wo_r = wout_d[0].rearrange("(c p) n -> p c n", p=128)
        wg_r = wg_d[l].rearrange("(c p) n -> p c n", p=128)
        wu_r = wu_d[l].rearrange("(c p) n -> p c n", p=128)
        wdn_r = wd_d[l].rearrange("(f p) n -> p f n", p=128)
        mh = tile(f"B{l}_mh", [128, DC, TB], BF16)
        xTb = tile(f"B{l}_xTb", [128, DC, TB], F32)
        yb = tile(f"B{l}_yb", [128, DC, TB], F32)
        actT = tile(f"B{l}_actT", [128, FC, TB], BF16)
        sqc = [tile(f"B{l}_sq{i}", [128, TB], BF16) for i in range(2)]
        rt = tile(f"B{l}_rt", [128, TB], F32)
        rstd = tile(f"B{l}_rstd", [128, TB], F32)
        tmpf = [tile(f"B{l}_tf{i}", [128, TB], F32) for i in range(2)]
        items_o, items_g, items_u, items_d = [], [], [], []
        for blk in range(NB):
            items_o += [wo_b[l][dc] for dc in range(DC)]
            items_g += [wg_b[l][f] for f in range(FC)]
            items_u += [wu_b[l][f] for f in range(FC)]
            for dc in range(DC):
                items_d += [wd_b[l][dc][:, 0:FC // 2, :], wd_b[l][dc][:, FC // 2:FC, :]]
        so = Stream(f"B{l}_wo", [128, DC, 128], items_o, nslot=3, nsplit=1, rres=[r_wB[l]])
        sg = Stream(f"B{l}_wg", [128, DC, 128], items_g, nslot=3, nsplit=1, rres=[r_wB[l]])
        su = Stream(f"B{l}_wu", [128, DC, 128], items_u, nslot=3, nsplit=1, rres=[r_wB[l]])
        sd = Stream(f"B{l}_wd", [128, FC // 2, 128], items_d, nslot=4, nsplit=1, rres=[r_wB[l]])
        bss = banks[7]

        def post_norm_residual(gg):
            act(rt[:, :], bss[:, :], AF.Sqrt, [bss], [rt], scale=1.0 / D, bias=EPS)
            recip(rstd[:, :], rt[:, :], [rt], [rstd])
            for dc in range(DC):
                stt(yb[:, dc, :], yb[:, dc, :], gg[:, dc:dc + 1], rstd[:, :], ALU.mult, ALU.mult, [yb, gg, rstd], [yb])
                tt("dve", xTb[:, dc, :], xTb[:, dc, :], yb[:, dc, :], ALU.add, [xTb, yb], [xTb])

        for blk in range(NB):
            load("sp", mh, mh[:, :, :], mixT_d[blk], [r_mixT[blk]])
            load("sp", xTb, xTb[:, :, :], xT_d[blk], [r_xT[blk]])
            for dc in range(DC):
                W = so.get(blk * DC + dc)
                b = nbank(0, 4)
                for c in range(DC):
                    mm(b, b[:, :], W[:, c, :], mh[:, c, :], c == 0, c == DC - 1, [W, mh])
                act(yb[:, dc, :], b[:, :], AF.Copy, [b], [yb])
                s = sqc[dc % 2]
                tt("dve", s[:, :], yb[:, dc, :], yb[:, dc, :], ALU.mult, [yb], [s])
                mm(bss, bss[:, :], ones_bf[:, :], s[:, :], dc == 0, dc == DC - 1, [ones_bf, s])
            post_norm_residual(ggm[l])
            norm_mod(xTb, TB, lambda c: sc1f[l][:, c:c + 1], lambda c: modc[l][:, 48 + c, 0:1],
                     mh, sqc, rt, rstd, tmpf, banks[6], [sc1f[l], modc[l]])
            for f in range(FC):
                Wg = sg.get(blk * FC + f)
                Wu = su.get(blk * FC + f)
                bg = nbank(0, 4)
                bu = nbank(0, 4)
                for c in range(DC):
                    mm(bg, bg[:, :], Wg[:, c, :], mh[:, c, :], c == 0, c == DC - 1, [Wg, mh])
                for c in range(DC):
                    mm(bu, bu[:, :], Wu[:, c, :], mh[:, c, :], c == 0, c == DC - 1, [Wu, mh])
                tf = tmpf[f % 2]
                act(tf[:, :], bg[:, :], AF.Silu, [bg], [tf])
                tt("dve", actT[:, f, :], tf[:, :], bu[:, :], ALU.mult, [tf, bu], [actT])
            for dc in range(DC):
                b = nbank(4, 6)
                for hf in range(2):
                    W = sd.get((blk * DC + dc) * 2 + hf)
                    for f2 in range(FC // 2):
                        f = hf * (FC // 2) + f2
                        mm(b, b[:, :], W[:, f2, :], actT[:, f, :], f == 0, f == FC - 1, [W, actT])
                act(yb[:, dc, :], b[:, :], AF.Copy, [b], [yb])
                s = sqc[dc % 2]
                tt("dve", s[:, :], yb[:, dc, :], yb[:, dc, :], ALU.mult, [yb], [s])
                mm(bss, bss[:, :], ones_bf[:, :], s[:, :], dc == 0, dc == DC - 1, [ones_bf, s])
            post_norm_residual(ggf[l])
            if not last:
                store(xTb, xT_d[blk], xTb[:, :, :], [r_xT[blk]])
            else:
                for t_ in range(4):
                    for c4 in range(4):
                        b = nbank(0, 4)
                        for i in range(4):
                            c = c4 * 4 + i
                            S.op("pe", lambda e, b=b, i=i, c=c, t_=t_: e.transpose(b[:, i * 128:(i + 1) * 128], xTb[:, c, t_ * 128:(t_ + 1) * 128], ident[:, :]),
                                 [xTb.res, ident.res], [b.res])
                        if c4 % 2 == 0:
                            act(yb[:, t_ * 4 + c4, :], b[:, :], AF.Copy, [b], [yb])
                        else:
                            S.op("dve", lambda e, b=b, c4=c4, t_=t_: e.tensor_copy(out=yb[:, t_ * 4 + c4, :], in_=b[:, :]), [b.res], [yb.res])
                    r0 = blk * TB + t_ * 128
                    store(yb, out_d[r0:r0 + 128, :].rearrange("p (a n) -> p a n", a=4), yb[:, t_ * 4:(t_ + 1) * 4, :], [r_out])

    def phaseC1():
        w_r = cdin_d[0].rearrange("(c p) n -> p c n", p=128)
        xTb = tile("C_xTb", [128, DC, TB], F32)
        hT = tile("C_hT", [128, DC, TB], BF16)
        sqc = [tile(f"C_sq{i}", [128, TB], BF16) for i in range(2)]
        rt = tile("C_rt", [128, TB], F32)
        rstd = tile("C_rstd", [128, TB], F32)
        tmpf = [tile(f"C_tf{i}", [128, TB], F32) for i in range(2)]
        sgm = [tile(f"C_sg{i}", [128, TB], F32) for i in range(2)]
        glu = [tile(f"C_glu{i}", [128, TB], F32) for i in range(2)]
        FT = [tile(f"C_FT{i}", [128, TB], BF16) for i in range(2)]
        GGb = [tile(f"C_GGb{i}", [128, 4, 256], BF16) for i in range(2)]
        ccs = tile("C_ccs", [128, 256], BF16)
        zt = tile("C_zt", [128, 16], F32)
        load("pool", ccs, ccs[:, :], ccs_d[:, :])
        S.op("dve", lambda e: e.memset(zt[:, :], 0.0), [], [zt.res])
        for j in range(8):
            store(zt, G_d[j][:, 0:16], zt[:, :], [r_G])
            store(zt, G_d[j][:, 16 + Tn:32 + Tn], zt[:, :], [r_G])
        items = []
        for blk in range(NB):
            items += [wC_b[k_] for k_ in range(24)]
        ws = Stream("C_w", [128, DC, 128], items, nslot=3, nsplit=1, rres=[r_wC])
        si = 0
        k = 0
        for blk in range(NB):
            load("sp", xTb, xTb[:, :, :], xT_d[blk], [r_xT[blk]])
            norm_mod(xTb, TB, lambda c: sc1m[1][:, c, 0:1], lambda c: modc[1][:, c, 0:1],
                     hT, sqc, rt, rstd, tmpf, banks[7], [sc1m[1], modc[1]])
            for j in range(8):
                Wa = ws.get(si)
                ba = nbank(0, 4)
                for c in range(DC):
                    mm(ba, ba[:, :], Wa[:, c, :], hT[:, c, :], c == 0, c == DC - 1, [Wa, hT])
                Wg = ws.get(si + 1)
                si += 2
                bg = nbank(0, 4)
                for c in range(DC):
                    mm(bg, bg[:, :], Wg[:, c, :], hT[:, c, :], c == 0, c == DC - 1, [Wg, hT])
                s_, g_ = sgm[k % 2], glu[k % 2]
                k += 1
                act(s_[:, :], bg[:, :], AF.Sigmoid, [bg], [s_])
                tt("dve", g_[:, :], s_[:, :], ba[:, :], ALU.mult, [s_, ba], [g_])
                store(g_, G_d[j][:, 16 + blk * TB:16 + (blk + 1) * TB], g_[:, :], [r_G])
            for g in range(8):
                Wf = ws.get(si)
                si += 1
                bf = nbank(0, 4)
                for c in range(DC):
                    mm(bf, bf[:, :], Wf[:, c, :], hT[:, c, :], c == 0, c == DC - 1, [Wf, hT])
                F_ = FT[g % 2]
                act(F_[:, :], bf[:, :], AF.Copy, [bf], [F_])
                GG = GGb[g % 2]
                for pr in range(2):
                    b2 = nbank(4, 6)
                    for tq in range(2):
                        t_ = pr * 2 + tq
                        mm(b2, b2[:, tq * 256:(tq + 1) * 256], F_[:, t_ * 128:(t_ + 1) * 128], ccs[:, :], True, True, [F_, ccs])
                    S.op("dve", lambda e, GG=GG, b2=b2, pr=pr: e.tensor_copy(out=GG[:, pr * 2:pr * 2 + 2, :], in_=b2[:, :].rearrange("p (a n) -> p a n", a=2)),
                         [b2.res], [GG.res])
                store(GG, GG_d[g][:, blk * 4:(blk + 1) * 4, :], GG[:, :, :], [r_GG])

    def phaseC2():
        yield
        tab = tile("F_tab", [128, NT, 2, TB], BF16)
        GGt = [tile(f"F_GG{i}", [128, NT, 256], BF16) for i in range(2)]
        yo = [tile(f"F_yo{i}", [128, TB], BF16) for i in range(2)]
        scale = 1.0 / float(np.sqrt(Tn * 128.0))
        it = 0
        for kb in range(NB):
            step = max(1, NT // 8)
            for a in range(0, NT, step):
                load("sp", tab, tab[:, a:a + step, :, :], dft_d[kb][:, a:a + step, :, :])
            for g in range(8):
                G_ = GGt[it % 2]
                load("sp", G_, G_[:, :, :], GG_d[g], [r_GG])
                b = nbank(0, 4)
                for nt in range(NT):
                    mm(b, b[:, :], G_[:, nt, 0:128], tab[:, nt, 0, :], nt == 0, False, [G_, tab])
                    mm(b, b[:, :], G_[:, nt, 128:256], tab[:, nt, 1, :], False, nt == NT - 1, [G_, tab])
                y_ = yo[it % 2]
                act(y_[:, :], b[:, :], AF.Copy, [b], [y_], scale=scale)
                store(y_, mixT_d[kb][:, 8 + g, :], y_[:, :], [r_mixT[kb]])
                it += 1
                yield

    def phaseC3():
        gin = [tile(f"V_gin{i}", [128, TB + 32], F32) for i in range(2)]
        yc = tile("V_yc", [128, 8, TB], F32)
        ybf = [tile(f"V_ybf{i}", [128, TB], BF16) for i in range(2)]
        ysq = [tile(f"V_ysq{i}", [128, TB], BF16) for i in range(2)]
        mu = tile("V_mu", [128, TB], F32)
        var = tile("V_var", [128, TB], F32)
        tmp = [tile(f"V_tmp{i}", [128, TB], F32) for i in range(2)]
        yo = tile("V_yo", [128, 8, TB], BF16)
        gbf = [tile(f"V_gbf{i}", [128, TB + 32], BF16) for i in range(2)]
        Dg = tile("V_Dg", [128, 8, 31, 128], BF16)
        dww = tile("V_dww", [128, 8, 31], F32)
        dwb = tile("V_dwb", [128, 8], F32)
        cng = tile("V_cng", [128, 8], F32)
        cnb = tile("V_cnb", [128, 8], F32)
        load("sp", dww, dww[:, :, :], dww_d[:, :, :])
        load("sp", dwb, dwb[:, :], dwb_d[:, :])
        load("sp", cng, cng[:, :], cng_d[:, :])
        load("sp", cnb, cnb[:, :], cnb_d[:, :])
        for j in range(8):
            for k in range(31):
                ts("dve", Dg[:, j, k, :], ident[:, :], dww[:, j, k:k + 1], None, ALU.mult, None, [ident, dww], [Dg])
        b1, b2 = banks[6], banks[7]
        it = 0
        for blk in range(NB):
            for j in range(8):
                it += 1
                g_ = gin[it % 2]
                load("sp", g_, g_[:, 0:TB + 30], G_d[j][:, 1 + blk * TB:1 + blk * TB + TB + 30], [r_G])
                gb_ = gbf[it % 2]
                act(gb_[:, 0:TB + 30], g_[:, 0:TB + 30], AF.Copy, [g_], [gb_])
                cb = nbank(4, 6)
                for k in range(31):
                    mm(cb, cb[:, :], Dg[:, j, k, :], gb_[:, k:k + TB], k == 0, k == 30, [Dg, gb_])
                yield
                act(yc[:, j, :], cb[:, :], AF.Identity, [cb, dwb], [yc], bias=dwb[:, j:j + 1])
                yb_, ys_ = ybf[j % 2], ysq[j % 2]
                act(yb_[:, :], yc[:, j, :], AF.Copy, [yc], [yb_])
                act(ys_[:, :], yc[:, j, :], AF.Square, [yc], [ys_])
                mm(b1, b1[:, :], ones_bf[:, :], yb_[:, :], j == 0, j == 7, [ones_bf, yb_])
                mm(b2, b2[:, :], ones_bf[:, :], ys_[:, :], j == 0, j == 7, [ones_bf, ys_])
            ts("dve", mu[:, :], b1[:, :], 1.0 / 1024, None, ALU.mult, None, [b1], [mu])
            tt("dve", var[:, :], mu[:, :], mu[:, :], ALU.mult, [mu], [var])
            stt(var[:, :], b2[:, :], 1.0 / 1024, var[:, :], ALU.mult, ALU.subtract, [b2, var], [var])
            act(var[:, :], var[:, :], AF.Sqrt, [var], [var], bias=EPS)
            recip(var[:, :], var[:, :], [var], [var])
            for j in range(8):
                t_ = tmp[j % 2]
                tt("dve", t_[:, :], yc[:, j, :], mu[:, :], ALU.subtract, [yc, mu], [t_])
                tt("dve", t_[:, :], t_[:, :], var[:, :], ALU.mult, [t_, var], [t_])
                act(yo[:, j, :], t_[:, :], AF.Silu, [t_, cng, cnb], [yo], scale=cng[:, j:j + 1], bias=cnb[:, j:j + 1])
            store(yo, mixT_d[blk][:, 0:8, :], yo[:, :, :], [r_mixT[blk]])

    def phaseC23():
        gens = [phaseC3(), phaseC2()]
        next(gens[1])
        while gens:
            for g_ in list(gens):
                try:
                    next(g_)
                except StopIteration:
                    gens.remove(g_)

    phases = [(phaseA,), (phaseATT,), (phaseB, 0, about_d, False), (phaseC1,), (phaseC23,), (phaseB, 1, cdout_d, True)]
    if only is not None:
        phases = [phases[i] for i in only]
    marks = [("phase0", S.q["pe"].n)]
    for ph in phases:
        with ExitStack() as st:
            stacks.append(st)
            ph[0](*ph[1:])
            S.barrier()
            stacks.pop()
        marks.append((ph[0].__name__ + str(ph[1] if len(ph) > 1 else ""), S.q["pe"].n))
    print("BUILD: sems", S.nsem, {k: q.n for k, q in S.q.items()}, "MARKS", marks)
    return nc


def _col(v, nch):
    return np.ascontiguousarray(np.asarray(v, np.float32).reshape(nch, 128).T)


def _consts(Tn):
    NB = Tn // TB
    NT = Tn // 128
    ident = np.eye(128, dtype=np.float32)
    P = np.zeros((128, 128), np.float32)
    for base in range(0, 128, 32):
        for i in range(16):
            P[base + i, base + i + 16] = 1.0
            P[base + i + 16, base + i] = 1.0
    permT = np.ascontiguousarray(P.T)
    n = np.arange(Tn)
    rows = (n // GRID_W).astype(np.float64)
    cols = (n % GRID_W).astype(np.float64)
    inv = 10000.0 ** (-np.arange(16, dtype=np.float64) / 16)
    ropeC = np.zeros((128, Tn), np.float64)
    ropeS = np.zeros((128, Tn), np.float64)
    for p in range(128):
        j = p % 64
        axis = j // 32
        f = j % 16
        second = (j % 32) >= 16
        ang = (rows if axis == 0 else cols) * inv[f]
        ropeC[p] = np.cos(ang)
        ropeS[p] = np.sin(ang) if second else -np.sin(ang)
    cc = np.arange(128, dtype=np.float64)
    B = 2 * np.pi * np.outer(cc, cc) / 128.0
    ccs = np.concatenate([np.cos(B), -np.sin(B)], axis=1).astype(np.float32)
    nn = np.arange(Tn, dtype=np.int64)
    prod = np.outer(nn, nn) % Tn
    A = 2 * np.pi * prod.astype(np.float64) / Tn
    Cn = np.cos(A).astype(np.float32)
    Sn = np.sin(A).astype(np.float32)
    dft = np.empty((NB, 128, NT, 2, TB), dtype=ml_dtypes.bfloat16)
    Cr = Cn.reshape(NT, 128, NB, TB).transpose(2, 1, 0, 3)
    Sr = Sn.reshape(NT, 128, NB, TB).transpose(2, 1, 0, 3)
    dft[:, :, :, 0, :] = Cr.astype(ml_dtypes.bfloat16)
    dft[:, :, :, 1, :] = Sr.astype(ml_dtypes.bfloat16)
    return dict(ident=ident, permT=permT, ropeC=ropeC.astype(np.float32), ropeS=ropeS.astype(np.float32),
                ccs=ccs, dft=dft)


def make_in_maps(inp, Tn, nb):
    f = lambda a: np.ascontiguousarray(np.asarray(a, np.float32))
    consts = _consts(Tn)
    bc = lambda v: np.ascontiguousarray(np.broadcast_to(np.asarray(v, np.float32), (128,) + np.asarray(v).shape))
    shared = dict(
        mod_w=f(inp["mod_w"]), ffn_w_gate=f(inp["ffn_w_gate"]), ffn_w_up=f(inp["ffn_w_up"]),
        ffn_w_down=f(inp["ffn_w_down"]), ab_w_in=f(inp["ab_w_in"]), ab_w_out=f(inp["ab_w_out"]),
        cd_w_in=f(inp["cd_w_in"]), cd_w_out=f(inp["cd_w_out"]),
        modb=np.ascontiguousarray(np.stack([_col(inp["mod_b"][l], 96) for l in range(2)], axis=1)),
        pmg=np.ascontiguousarray(np.stack([_col(inp["post_mix_g"][l], 16) for l in range(2)], axis=1)),
        pfg=np.ascontiguousarray(np.stack([_col(inp["post_ffn_g"][l], 16) for l in range(2)], axis=1)),
        lamv=bc(np.stack([inp["ab_lam_q1"][0], inp["ab_lam_k1"][0], inp["ab_lam_q2"][0], inp["ab_lam_k2"][0]])),
        subg=f(np.asarray(inp["ab_subln_g"][0]).reshape(128, 1)),
        vng=bc(inp["ab_vnorm_g"][0]), vnb=bc(inp["ab_vnorm_b"][0]),
        bsp=bc(inp["ab_b_spatial"][0]),
        wsT=f(np.transpose(np.asarray(inp["ab_w_spatial"][0]), (2, 0, 1))),
        dww=f(np.transpose(np.asarray(inp["cd_dw_w"][0]).reshape(31, 8, 128), (2, 1, 0))),
        dwb=_col(inp["cd_dw_b"][0], 8), cng=_col(inp["cd_norm_g"][0], 8), cnb=_col(inp["cd_norm_b"][0], 8),
        **consts,
    )
    maps = []
    for b in range(nb):
        m = dict(shared)
        m["x"] = f(inp["x"][b][:Tn])
        m["ctx"] = f(inp["ctx"][b])
        m["cvec"] = np.ascontiguousarray(np.stack([_col(inp["c"][b], 16), _col(inp["c_ctx"], 16)], axis=2))
        maps.append(m)
    return maps


def kernel(**inputs):
    Tn = 4096
    nb = 4
    nc = build(Tn)
    maps = make_in_maps(inputs, Tn, nb)
    consts = ("ident", "permT", "ropeC", "ropeS", "ccs", "dft")
    zmap = {k: (v if k in consts else np.zeros_like(v)) for k, v in maps[0].items()}
    in_maps = [maps[i] if i < nb else zmap for i in range(8)]
    res = run_bass_kernel_spmd(nc, in_maps, core_ids=list(range(8)))
    return np.stack([np.asarray(res.results[b]["out"], np.float32) for b in range(nb)], axis=0)
```
